# Optimizing a Trainium2 kernel written in Bass

```python
import math
import jax, jax.numpy as jnp
from jax import lax
import numpy as np

D_MODEL = 2048
BATCH = 2
SEQ = 8192
DEPTH = 1
DEC_BATCH = 128
DEC_SEQ = 8
PAST_LEN = 16384
PAGE_SIZE = 128

HEAD_DIM = 64
D_ATTN = D_MODEL // 2
N_HEADS = D_ATTN // HEAD_DIM
KV_HEADS = 4
GROUP = N_HEADS // KV_HEADS
KV_DIM = KV_HEADS * HEAD_DIM
WINDOW = 128
D_CONV = D_MODEL - D_ATTN
CONV_W = 3
D_FF = 4 * D_MODEL
N_BUCKETS = 32
MAX_EXACT = N_BUCKETS // 2
REL_MAX_DIST = 128
D_IN = D_ATTN + 2 * KV_DIM + 3 * D_CONV
ALPHA = (2.0 * DEPTH) ** 0.25
BETA = (8.0 * DEPTH) ** -0.25
LN_EPS = 1e-5
NEG = -1e30

kernel_name = "hymba_swa_sink_shortconv_deepnorm_adaln"


def _ln(x):
    xf = x.astype(jnp.float32)
    mu = jnp.mean(xf, axis=-1, keepdims=True)
    var = jnp.mean(jnp.square(xf - mu), axis=-1, keepdims=True)
    return (xf - mu) * lax.rsqrt(var + LN_EPS)


def _rel_bucket(dist):
    n = jnp.maximum(dist, 0)
    nf = jnp.maximum(n, 1).astype(jnp.float32)
    large = MAX_EXACT + (jnp.log(nf / MAX_EXACT) / math.log(REL_MAX_DIST / MAX_EXACT)
                         * (N_BUCKETS - MAX_EXACT)).astype(jnp.int32)
    large = jnp.minimum(large, N_BUCKETS - 1)
    return jnp.where(n < MAX_EXACT, n, large)


def _rel_bias(rel_table, dist):
    b = jnp.take(rel_table, _rel_bucket(dist), axis=0)
    b = jnp.transpose(b, (2, 0, 1)).astype(jnp.float32)
    return b.reshape(KV_HEADS, GROUP, dist.shape[0], dist.shape[1])


def _sink_attention(q, k, v, bias, mask, sinks):
    s = jnp.einsum('...qhgd,...khd->...hgqk', q, k).astype(jnp.float32) * (HEAD_DIM ** -0.5) + bias
    s = jnp.where(mask, s, NEG)
    sk = sinks.astype(jnp.float32).reshape(KV_HEADS, GROUP, 1, 1)
    m = jnp.maximum(jnp.max(s, axis=-1, keepdims=True), sk)
    p = jnp.exp(s - m)
    p = p / (jnp.sum(p, axis=-1, keepdims=True) + jnp.exp(sk - m))
    return jnp.einsum('...hgqk,...khd->...qhgd', p.astype(v.dtype), v)


def _attn_prompt(q, k, v, rel_table, sinks):
    B, T = q.shape[0], q.shape[1]
    nb = T // WINDOW
    qb = q.reshape(B, nb, WINDOW, KV_HEADS, GROUP, HEAD_DIM)
    kp = jnp.concatenate([jnp.zeros_like(k[:, :WINDOW]), k], axis=1).reshape(B, nb + 1, WINDOW, KV_HEADS, HEAD_DIM)
    vp = jnp.concatenate([jnp.zeros_like(v[:, :WINDOW]), v], axis=1).reshape(B, nb + 1, WINDOW, KV_HEADS, HEAD_DIM)
    kb = jnp.concatenate([kp[:, :-1], kp[:, 1:]], axis=2)
    vb = jnp.concatenate([vp[:, :-1], vp[:, 1:]], axis=2)
    qi = jnp.arange(WINDOW, dtype=jnp.int32)[:, None]
    ki = jnp.arange(2 * WINDOW, dtype=jnp.int32)[None, :]
    dist = qi + WINDOW - ki
    band = (dist >= 0) & (dist < WINDOW)
    kpos = jnp.arange(nb, dtype=jnp.int32)[:, None, None] * WINDOW - WINDOW + ki[None]
    mask = (band[None] & (kpos >= 0))[:, None, None]
    o = _sink_attention(qb, kb, vb, _rel_bias(rel_table, dist), mask, sinks)
    return o.reshape(B, T, D_ATTN), k[:, T - WINDOW:], v[:, T - WINDOW:]


def _attn_sample(q, k, v, cache_k, cache_v, rel_table, sinks):
    S = q.shape[1]
    Wb = cache_k.shape[1]
    kc = jnp.concatenate([cache_k.astype(k.dtype), k], axis=1)
    vc = jnp.concatenate([cache_v.astype(v.dtype), v], axis=1)
    dist = (Wb + jnp.arange(S, dtype=jnp.int32))[:, None] - jnp.arange(Wb + S, dtype=jnp.int32)[None, :]
    mask = (dist >= 0) & (dist < WINDOW)
    o = _sink_attention(q, kc, vc, _rel_bias(rel_table, dist), mask, sinks)
    return o.reshape(q.shape[0], S, D_ATTN), kc[:, S:], vc[:, S:]


def _short_conv(u, buf, conv_w):
    T = u.shape[1]
    up = jnp.concatenate([buf.astype(u.dtype), u], axis=1)
    out = conv_w[0] * up[:, 0:T]
    for i in range(1, CONV_W):
        out = out + conv_w[i] * up[:, i:i + T]
    return out, up[:, T:]


def _layer(x, c, attn_fn, conv_buf, w_ada, b_ada, w_in, conv_w, w_o, ln1_g, ln1_b, w_up, w_down, ln2_g, ln2_b):
    N, T = x.shape[0], x.shape[1]
    mod = (jax.nn.silu(c) @ w_ada + b_ada).astype(jnp.float32)[:, None, :]
    sh1, sc1, g1, sh2, sc2, g2 = jnp.split(mod, 6, axis=-1)
    h = (_ln(x) * (1.0 + sc1) + sh1).astype(x.dtype)
    proj = h @ w_in
    q, k, v, gb, gc, xc = jnp.split(
        proj, [D_ATTN, D_ATTN + KV_DIM, D_ATTN + 2 * KV_DIM,
               D_ATTN + 2 * KV_DIM + D_CONV, D_ATTN + 2 * KV_DIM + 2 * D_CONV], axis=-1)
    q = q.reshape(N, T, KV_HEADS, GROUP, HEAD_DIM)
    k = k.reshape(N, T, KV_HEADS, HEAD_DIM)
    v = v.reshape(N, T, KV_HEADS, HEAD_DIM)
    o_attn, k_new, v_new = attn_fn(q, k, v)
    conv_out, conv_new = _short_conv(gc * xc, conv_buf, conv_w)
    mix = jnp.concatenate([o_attn, gb * conv_out], axis=-1) @ w_o
    x1 = (_ln(ALPHA * x.astype(jnp.float32) + g1 * mix.astype(jnp.float32)) * ln1_g + ln1_b).astype(x.dtype)
    h2 = (_ln(x1) * (1.0 + sc2) + sh2).astype(x.dtype)
    ff = jnp.square(jax.nn.relu(h2 @ w_up)) @ w_down
    x2 = (_ln(ALPHA * x1.astype(jnp.float32) + g2 * ff.astype(jnp.float32)) * ln2_g + ln2_b).astype(x.dtype)
    return x2, k_new, v_new, conv_new


def setup_inputs(seed: int = 0) -> dict:
    key = jax.random.key(seed)
    ks = jax.random.split(key, 20)
    f32 = jnp.float32
    win_buf = min(WINDOW, PAST_LEN)
    col_scale = jnp.concatenate([
        jnp.ones((D_ATTN + KV_DIM,), f32), jnp.full((KV_DIM,), BETA, f32),
        jnp.ones((2 * D_CONV,), f32), jnp.full((D_CONV,), BETA, f32)])
    return {
        "x_prompt": jax.random.normal(ks[0], (BATCH, SEQ, D_MODEL), f32),
        "x_sample": jax.random.normal(ks[1], (DEC_BATCH, DEC_SEQ, D_MODEL), f32),
        "cache_k": jax.random.normal(ks[2], (DEC_BATCH, win_buf, KV_HEADS, HEAD_DIM), f32),
        "cache_v": jax.random.normal(ks[3], (DEC_BATCH, win_buf, KV_HEADS, HEAD_DIM), f32),
        "state_conv": jax.random.normal(ks[4], (DEC_BATCH, CONV_W - 1, D_CONV), f32),
        "c_prompt": jax.random.normal(ks[5], (BATCH, D_MODEL), f32),
        "c_sample": jax.random.normal(ks[6], (DEC_BATCH, D_MODEL), f32),
        "w_ada": jax.random.normal(ks[7], (D_MODEL, 6 * D_MODEL), f32) * (0.5 * D_MODEL ** -0.5),
        "b_ada": 0.01 * jax.random.normal(ks[8], (6 * D_MODEL,), f32),
        "w_in": jax.random.normal(ks[9], (D_MODEL, D_IN), f32) * (D_MODEL ** -0.5) * col_scale,
        "attn_sinks": 0.5 * jax.random.normal(ks[10], (N_HEADS,), f32),
        "rel_bias": 0.5 * jax.random.normal(ks[11], (N_BUCKETS, N_HEADS), f32),
        "conv_w": jax.random.normal(ks[12], (CONV_W, D_CONV), f32) * (CONV_W ** -0.5),
        "w_o": jax.random.normal(ks[13], (D_MODEL, D_MODEL), f32) * (D_MODEL ** -0.5) * BETA,
        "ln1_g": 1.0 + 0.02 * jax.random.normal(ks[14], (D_MODEL,), f32),
        "ln1_b": 0.02 * jax.random.normal(ks[15], (D_MODEL,), f32),
        "w_up": jax.random.normal(ks[16], (D_MODEL, D_FF), f32) * (D_MODEL ** -0.5) * BETA,
        "w_down": jax.random.normal(ks[17], (D_FF, D_MODEL), f32) * (D_FF ** -0.5) * BETA,
        "ln2_g": 1.0 + 0.02 * jax.random.normal(ks[18], (D_MODEL,), f32),
        "ln2_b": 0.02 * jax.random.normal(ks[19], (D_MODEL,), f32),
    }


def reference(x_prompt, x_sample, cache_k, cache_v, state_conv, c_prompt, c_sample,
              w_ada, b_ada, w_in, attn_sinks, rel_bias, conv_w, w_o, ln1_g, ln1_b,
              w_up, w_down, ln2_g, ln2_b):
    yp = x_prompt
    for _ in range(DEPTH):
        conv_buf_p = jnp.zeros((x_prompt.shape[0], CONV_W - 1, D_CONV), x_prompt.dtype)
        yp, k_prompt, v_prompt, conv_prompt = _layer(
            yp, c_prompt,
            lambda q, k, v: _attn_prompt(q, k, v, rel_bias, attn_sinks),
            conv_buf_p, w_ada, b_ada, w_in, conv_w, w_o, ln1_g, ln1_b, w_up, w_down, ln2_g, ln2_b)
    ys = x_sample
    for _ in range(DEPTH):
        ys, k_sample, v_sample, conv_sample = _layer(
            ys, c_sample,
            lambda q, k, v: _attn_sample(q, k, v, cache_k, cache_v, rel_bias, attn_sinks),
            state_conv, w_ada, b_ada, w_in, conv_w, w_o, ln1_g, ln1_b, w_up, w_down, ln2_g, ln2_b)
    return (yp, ys, k_prompt, v_prompt, conv_prompt, k_sample, v_sample, conv_sample)
```

```python
import math
from contextlib import ExitStack

import numpy as np
import concourse.bass as bass
import concourse.mybir as mybir
from concourse.bass_utils import run_bass_kernel_spmd

F32 = mybir.dt.float32
BF16 = mybir.dt.bfloat16
ACTF = mybir.ActivationFunctionType
ALU = mybir.AluOpType
AX = mybir.AxisListType

D = 2048
DFF = 8192
NEG = -1e30
ALPHA = 2.0 ** 0.25
EPS = 1e-5
ENGS = ("pe", "act", "dve", "pool", "sp")


class Prog:
    def __init__(self, nc, stack, ndma=8):
        self.nc = nc
        self.q = {e: [] for e in ENGS}
        self.sem = {}
        self.cnt = {}
        for e in ("pe", "act", "dve", "pool"):
            self.sem[e] = stack.enter_context(nc.semaphore("s_" + e))
            self.cnt[e] = 0
        self.dsem = {}
        self.dnext = {}
        for e in ("sp", "pool"):
            self.dsem[e] = []
            for i in range(ndma):
                k = "d_%s%d" % (e, i)
                self.sem[k] = stack.enter_context(nc.semaphore(k))
                self.cnt[k] = 0
                self.dsem[e].append(k)
            self.dnext[e] = 0
        self.bar = stack.enter_context(nc.semaphore("s_bar"))
        self.barcnt = 0
        self.waited = {e: {} for e in ENGS}
        self.lastw = {}
        self.reads = {}
        self.range = {}

    def alias(self, name, lo, n):
        self.range[name] = (lo, lo + n)

    def _overl(self, w):
        if w not in self.range:
            return ()
        lo, hi = self.range[w]
        return [r for r, (a, b) in self.range.items() if r != w and a < hi and lo < b]

    def _need(self, eng, ev, out):
        if ev is None:
            return
        k, v = ev
        if k == eng and (eng == "pe" or v > self.cnt[eng]):
            return
        if self.waited[eng].get(k, 0) >= v:
            return
        if out.get(k, 0) < v:
            out[k] = v

    def _deps(self, eng, reads, writes):
        need = {}
        for r in reads:
            self._need(eng, self.lastw.get(r), need)
        for w in writes:
            for w2 in [w] + list(self._overl(w)):
                self._need(eng, self.lastw.get(w2), need)
                for ev in self.reads.get(w2, ()):
                    self._need(eng, ev, need)
        for k, v in need.items():
            self.waited[eng][k] = v
        return list(need.items())

    def _commit(self, ev, reads, writes):
        for r in reads:
            self.reads.setdefault(r, []).append(ev)
        for w in writes:
            self.lastw[w] = ev
            self.reads[w] = []

    def op(self, eng, fn, reads=(), writes=(), inc=True):
        waits = self._deps(eng, reads, writes)
        if inc:
            self.cnt[eng] += 1
            ev = (eng, self.cnt[eng])
        else:
            ev = (eng, self.cnt[eng] + 1)
        self.q[eng].append((waits, fn, (eng, 1) if inc else None))
        self._commit(ev, reads, writes)
        return ev

    def dma(self, eng, fn, reads=(), writes=()):
        ring = self.dsem[eng]
        k = ring[self.dnext[eng] % len(ring)]
        self.dnext[eng] += 1
        waits = self._deps(eng, reads, writes)
        if self.waited[eng].get(k, 0) < self.cnt[k]:
            waits.append((k, self.cnt[k]))
            self.waited[eng][k] = self.cnt[k]
        self.cnt[k] += 16
        ev = (k, self.cnt[k])
        self.q[eng].append((waits, fn, (k, 16)))
        self._commit(ev, reads, writes)
        return ev

    def barrier(self):
        for e in ENGS:
            waits = []
            if e in self.dsem:
                for k in self.dsem[e]:
                    if self.waited[e].get(k, 0) < self.cnt[k]:
                        waits.append((k, self.cnt[k]))
                        self.waited[e][k] = self.cnt[k]
            if e in self.cnt and self.cnt[e] > 0 and self.waited[e].get(e, 0) < self.cnt[e]:
                waits.append((e, self.cnt[e]))
                self.waited[e][e] = self.cnt[e]
            self.q[e].append((waits, "BAR_INC", None))
        self.barcnt += len(ENGS)
        for e in ENGS:
            self.q[e].append(([("__bar", self.barcnt)], None, None))
        self.lastw = {}
        self.reads = {}

    def run(self):
        nc = self.nc
        prog = self

        def replay(ename, eng):
            for waits, fn, inc in prog.q[ename]:
                for k, v in waits:
                    if k == "__bar":
                        eng.wait_ge(prog.bar, v)
                    else:
                        eng.wait_ge(prog.sem[k], v)
                if fn is None:
                    continue
                if fn == "BAR_INC":
                    eng.nop().then_inc(prog.bar, 1)
                    continue
                ins = fn(eng)
                if inc is not None:
                    ins.then_inc(prog.sem[inc[0]], inc[1])

        with nc.Block() as block:
            @block.tensor
            def _(e):
                replay("pe", e)

            @block.scalar
            def _(e):
                replay("act", e)

            @block.vector
            def _(e):
                replay("dve", e)

            @block.gpsimd
            def _(e):
                replay("pool", e)

            @block.sync
            def _(e):
                replay("sp", e)


def head_loc(h):
    if h < 4:
        return h, 0, 0
    if h < 8:
        return h - 4, 64, 0
    if h < 12:
        return 4 + (h - 8), 0, 1
    return 4 + (h - 12), 64, 1


def build(NPT=16, NT=4):
    nc = bass.Bass("TRN2", target_bir_lowering=False)
    NTILES = NPT + 2
    C = NT * 128

    def din(name, shape):
        return nc.dram_tensor(name, list(shape), F32, kind="ExternalInput").ap()

    def dout(name, shape):
        return nc.dram_tensor(name, list(shape), F32, kind="ExternalOutput").ap()

    x_all = din("x_all", [NTILES, 128, D])
    cT_d = din("cT", [128, 16, 17])
    w_adaB = din("w_adaB", [64, 128, 16, 128])
    w_adaG = din("w_adaG", [2, D, D])
    b_adaT = din("b_adaT", [128, 64])
    b_adaG = din("b_adaG", [2, D])
    w_inB = din("w_inB", [34, 128, 16, 128])
    w_kv = din("w_kv", [D, 512])
    w_o = din("w_o", [D, D])
    w_upB = din("w_upB", [64, 128, 16, 128])
    w_down = din("w_down", [DFF, D])
    bias_d = din("bias_tab", [128, 16, 256])
    prevmask_d = din("prevmask", [128, 128])
    sbias_d = din("sbias", [128, 136])
    sinkbc_d = din("sink_bc", [128, 16])
    sinks_d = din("sink_s", [128, 1])
    convw_d = din("conv_wT", [128, 8, 3])
    state_d = din("stateT", [128, 8, 16, 2])
    ident_d = din("ident", [128, 128])
    sel_d = din("sel", [128, 2, 64])
    lnp_d = din("lnp", [4, D])
    flag_d = din("flag", [128, 1])
    cache_k = din("cache_k", [16, 128, 256])
    cache_v = din("cache_v", [16, 128, 256])

    y_all = dout("y_all", [NPT + 1, 128, D])
    kv_last = dout("kv_last", [128, 512])
    conv_last = dout("conv_last", [2, 1024])
    ks_out = dout("ks_out", [16, 128, 256])
    vs_out = dout("vs_out", [16, 128, 256])
    convs_out = dout("convs_out", [32, 1024])
    gscr = nc.dram_tensor("gscr", [17, 2, D], F32, kind="Internal").ap()

    st = ExitStack()
    with st:
        cur = [(nc._sbuf_addr_for_side("left") + 63) // 64 * 64]
        lim = nc._sbuf_addr_for_side("right")

        def sb(name, shape, dt, at=None):
            nbytes = int(np.prod(shape[1:])) * (4 if dt == F32 else 2)
            nbytes = (nbytes + 31) // 32 * 32
            if at is None:
                off = cur[0]
                cur[0] += nbytes
                assert cur[0] <= lim, ("SBUF overflow", name, cur[0], lim)
            else:
                off = at
            return nc.alloc_sbuf_tensor_at(name, list(shape), dt, offset=off), off, nbytes

        def sbp(name, shape, dt):
            return sb(name, shape, dt)[0]

        identb = sbp("identb", [128, 128], BF16)
        identf = sbp("identf", [128, 128], F32)
        modT = sbp("modT", [128, 64, 17], F32)
        bias_t = sbp("bias_t", [128, 16, 256], F32)
        sbias = sbp("sbias", [128, 136], F32)
        sink_bc = sbp("sink_bcs", [128, 16], F32)
        sink_s = sbp("sink_ss", [128, 1], F32)
        prevmask = sbp("prevmasks", [128, 128], F32)
        convw = sbp("convw", [128, 8, 3], F32)
        flag = sbp("flags", [128, 1], F32)
        epsb = sbp("epsb", [128, 1], F32)
        epsb2 = sbp("epsb2", [128, 1], F32)
        carry = sbp("carry", [128, 8, 2], F32)
        kT = sbp("kT", [128, 2, 128 + C], BF16)
        Vb = sbp("Vb", [128, NT + 1, 256], BF16)
        NSL = NT + 1
        stt = sbp("stt", [128, NSL, 4, 6], F32)
        mv = sbp("mv", [128, NSL, 2], F32)
        rstd = sbp("rstd", [128, NSL], F32)
        nb = sbp("nb", [128, NSL], F32)
        xnb2 = sbp("xnb2", [128, D], BF16)
        mx = sbp("mx", [128, 16], F32)
        negm = sbp("negm", [128, 16], F32)
        sm = sbp("sm", [128, 16], F32)
        rc = sbp("rc", [128, 16], F32)
        bc = sbp("bc", [128, 4, D], F32)
        xs = sbp("xs", [128, D], F32)
        xnb = sbp("xnb", [128, D], BF16)
        wB = sbp("wB", [128, 4, 16, 128], BF16)
        wA = sbp("wA", [128, 4, 4, 512], BF16)
        hT = sbp("hT", [128, 16, C], BF16)
        mixT = sbp("mixT", [128, 16, C], BF16)
        rT = sbp("rT", [128, C], BF16)
        tmpq, TQ, TQn = sb("tmpq", [128, 512], F32)
        acc, RA, RAn = sb("acc", [128, NT, D], F32)
        uT, RU, RUn = sb("uT", [128, 32, C], BF16)
        print("SBUF used", cur[0], "of", lim)

        o = [RA]

        ranges = {}

        def ra(name, shape, dt):
            t, off, n = sb(name, shape, dt, at=o[0])
            ranges[name] = (off, n)
            o[0] += n
            assert o[0] <= RA + RAn, ("RA overflow", name)
            return t

        gcs = ra("gcs", [128, C], F32)
        ubuf = ra("ubuf", [128, C + 2], F32)
        tcv = ra("tcv", [128, C], F32)
        us = ra("us", [128, 16, 10], F32)
        ts = ra("ts", [128, 16, 8], F32)
        kvst = ra("kvst", [128, 512], F32)
        ulast_s = ra("ulast_s", [128, 8, 16, 2], F32)
        ulast_p = ra("ulast_p", [128, 8, 2], F32)
        cvst = ra("cvst", [32, 1024], F32)
        o[0] = RA
        ck = ra("ck", [128, 8, 256], BF16)
        cv = ra("cv", [128, 8, 256], BF16)
        KT = ra("KT", [128, 8, 2, 136], BF16)
        Ss = ra("Ss", [128, 8, 138], F32)
        Pf = ra("Pf", [128, 8, 138], F32)
        Pb = ra("Pb", [128, 8, 136], BF16)
        PTc = ra("PTc", [128, 8, 128], BF16)
        PTn = ra("PTn", [8, 8, 128], BF16)
        vnew = ra("vnew", [8, 8, 256], BF16)
        osb = ra("osb", [128, 8, 64], BF16)
        o[0] = RA
        cTs = ra("cTs", [128, 16, 17], F32)
        siluT = ra("siluT", [128, 16, 17], BF16)
        badaT = ra("badaT", [128, 64], F32)
        bG = ra("bG", [17, 2, D], F32)

        o[0] = RU

        def ru(name, shape, dt):
            t, off, n = sb(name, shape, dt, at=o[0])
            ranges[name] = (off, n)
            o[0] += n
            assert o[0] <= RU + RUn, ("RU overflow", name)
            return t

        PT2 = sb("PT2", [128, 4, 2, 128], BF16, at=TQ)[0]
        ranges["PT2"] = (TQ, 2048)
        ranges["tmpq"] = (TQ, 2048)
        qT = ru("qT", [128, 8, C], BF16)
        S = ru("S", [128, 8, 258], F32)
        Pe = ru("Pe", [128, 16, 258], BF16)
        PT = ru("PT", [128, 4, 2, 128], BF16)
        Abuf = ru("Abuf", [128, 1024], BF16)
        qs = ru("qs", [128, 2, 16, 32], BF16)
        selb = ru("selb", [128, 2, 64], BF16)
        o[0] = RU
        gs = ru("gs", [17, 2, D], F32)

        psb = [st.enter_context(nc.psum_tensor("psb%d" % i, [128, 512], F32)) for i in range(8)]
        pscur = [0]
        held = set()

        def nextbank():
            while True:
                b = pscur[0] % 8
                pscur[0] += 1
                if b not in held:
                    return b

        P = Prog(nc, st)
        for name, (off, n) in ranges.items():
            if name == "qT":
                for m in range(8):
                    P.alias("qT%d" % m, off + m * C * 2, C * 2)
            elif name == "gs":
                for v in range(2):
                    P.alias("gs%d" % v, off, n)
            elif name == "S":
                for par in range(2):
                    P.alias("S%d" % par, off + par * (n // 2), n // 2)
            elif name == "Pe":
                for par in range(4):
                    P.alias("Pe%d" % par, off + par * (n // 4), n // 4)
            else:
                P.alias(name, off, n)
        for j in range(NT):
            P.alias("acc%d" % j, RA + j * D * 4, D * 4)
        for m in range(32):
            P.alias("uT%d" % m, RU + m * C * 2, C * 2)

        P.dma("sp", lambda e: e.dma_start(out=identf[:], in_=ident_d), writes=["identf"])
        P.dma("pool", lambda e: e.dma_start(out=identb[:], in_=ident_d), writes=["identb"])
        P.dma("sp", lambda e: e.dma_start(out=bias_t[:], in_=bias_d), writes=["bias_t"])
        P.dma("sp", lambda e: e.dma_start(out=sbias[:], in_=sbias_d), writes=["sbias"])
        P.dma("sp", lambda e: e.dma_start(out=sink_bc[:], in_=sinkbc_d), writes=["sink_bc"])
        P.dma("sp", lambda e: e.dma_start(out=sink_s[:], in_=sinks_d), writes=["sink_s"])
        P.dma("sp", lambda e: e.dma_start(out=prevmask[:], in_=prevmask_d), writes=["prevmask"])
        P.dma("sp", lambda e: e.dma_start(out=convw[:], in_=convw_d), writes=["convw"])
        P.dma("sp", lambda e: e.dma_start(out=flag[:], in_=flag_d), writes=["flag"])
        P.op("dve", lambda e: e.memset(epsb[:], EPS), writes=["epsb"])
        P.op("dve", lambda e: e.memset(epsb2[:], EPS / (ALPHA * ALPHA)), writes=["epsb2"])
        P.op("dve", lambda e: e.memset(carry[:], 0.0), writes=["carry"])

        def stageB(wd, chunk_ids, rhs_fn, rhs_res, ncols_fn, epilogue, post_hook=None):
            n = len(chunk_ids)
            PF = 3

            def issue(i):
                s = i % 4
                P.dma("pool", lambda e, i=i, s=s: e.dma_start(out=wB[:, s], in_=wd[chunk_ids[i]]),
                      writes=["wB%d" % s])

            for i in range(min(PF, n)):
                issue(i)
            for i in range(n):
                if i + PF < n:
                    issue(i + PF)
                s = i % 4
                b = nextbank()
                for kc in range(16):
                    P.op("pe", lambda e, b=b, s=s, kc=kc: e.matmul(
                        ncols_fn(psb[b]), lhsT=wB[:, s, kc, :], rhs=rhs_fn(kc),
                        start=(kc == 0), stop=(kc == 15)),
                        reads=["wB%d" % s] + rhs_res, writes=["ps%d" % b], inc=(kc == 15))
                epilogue(i, chunk_ids[i], b)
                if post_hook is not None:
                    post_hook(i)

        def stageA(wd, row0, nk, tiles, lhs_fn, lhs_res_fn, epilogue, out_fn=lambda ps: ps[:, :], cq_hook=None, ncq=4):
            ng = nk // 4
            for cq in range(ncq):
                if cq_hook is not None:
                    cq_hook(cq)
                banks = [nextbank() for _ in tiles]

                def issue(g, cq=cq):
                    s = g % 4
                    src = wd[row0 + g * 512: row0 + (g + 1) * 512, cq * 512:(cq + 1) * 512]
                    P.dma("pool", lambda e, s=s, src=src: e.dma_start(
                        out=wA[:, s], in_=src.rearrange("(g p) n -> p g n", p=128)),
                        writes=["wA%d" % s])

                for g in range(min(3, ng)):
                    issue(g)
                for g in range(ng):
                    if g + 3 < ng:
                        issue(g + 3)
                    s = g % 4
                    for k4 in range(4):
                        kc = g * 4 + k4
                        for ti, j in enumerate(tiles):
                            P.op("pe", lambda e, b=banks[ti], s=s, k4=k4, kc=kc, j=j: e.matmul(
                                out_fn(psb[b]), lhsT=lhs_fn(kc, j), rhs=wA[:, s, k4, :],
                                start=(kc == 0), stop=(kc == nk - 1)),
                                reads=["wA%d" % s] + lhs_res_fn(kc, j), writes=["ps%d" % banks[ti]],
                                inc=(kc == nk - 1) or (k4 == 3 and ti == len(tiles) - 1))
                for ti, j in enumerate(tiles):
                    epilogue(j, cq, banks[ti])

        def ln_stats(src_ap, src_res, sl=0):
            t = "_%d" % sl
            for i in range(4):
                P.op("dve", lambda e, i=i: e.bn_stats(out=stt[:, sl, i, :], in_=src_ap[:, i * 512:(i + 1) * 512]),
                     reads=src_res, writes=["stt%d" % i + t])
            P.op("dve", lambda e: e.bn_aggr(out=mv[:, sl, :], in_=stt[:, sl].rearrange("p a b -> p (a b)")),
                 reads=["stt%d" % i + t for i in range(4)], writes=["mv" + t])
            P.op("act", lambda e: e.activation(out=rstd[:, sl:sl + 1], in_=mv[:, sl, 1:2], func=ACTF.Sqrt, bias=epsb[:, 0:1]),
                 reads=["mv" + t, "epsb"], writes=["rstd" + t])
            P.op("dve", lambda e: e.reciprocal(out=rstd[:, sl:sl + 1], in_=rstd[:, sl:sl + 1]),
                 reads=["rstd" + t], writes=["rstd" + t])
            P.op("dve", lambda e: e.tensor_scalar(out=nb[:, sl:sl + 1], in0=mv[:, sl, 0:1], scalar1=rstd[:, sl:sl + 1],
                                                  scalar2=-1.0, op0=ALU.mult, op1=ALU.mult),
                 reads=["mv" + t, "rstd" + t], writes=["nb" + t])

        def ln_stats_multi(tl, src_fn, res_fn, eps_t=None, eps_r="epsb"):
            et = epsb if eps_t is None else eps_t
            for j in tl:
                t = "_%d" % j
                src = src_fn(j)
                for i in range(4):
                    P.op("dve", lambda e, i=i, j=j, src=src: e.bn_stats(out=stt[:, j, i, :], in_=src[:, i * 512:(i + 1) * 512]),
                         reads=res_fn(j), writes=["stt%d" % i + t])
                P.op("dve", lambda e, j=j: e.bn_aggr(out=mv[:, j, :], in_=stt[:, j].rearrange("p a b -> p (a b)")),
                     reads=["stt%d" % i + t for i in range(4)], writes=["mv" + t])
            lo, hi = tl[0], tl[-1] + 1
            mvr = ["mv_%d" % j for j in tl]
            rr = ["rstd_%d" % j for j in tl]
            nr = ["nb_%d" % j for j in tl]
            P.op("act", lambda e: e.activation(out=rstd[:, lo:hi], in_=mv[:, lo:hi, 1], func=ACTF.Sqrt, bias=et[:, 0:1]),
                 reads=mvr + [eps_r], writes=rr)
            P.op("dve", lambda e: e.reciprocal(out=rstd[:, lo:hi], in_=rstd[:, lo:hi]), reads=rr, writes=rr)
            P.op("dve", lambda e: e.scalar_tensor_tensor(out=nb[:, lo:hi], in0=mv[:, lo:hi, 0], scalar=-1.0, in1=rstd[:, lo:hi],
                                                         op0=ALU.mult, op1=ALU.mult),
                 reads=mvr + rr, writes=nr)

        xnbs = [xnb, xnb2]

        def norm_A(src_ap, src_res, sl=0, xb=0, stats=True):
            t = "_%d" % sl
            if stats:
                ln_stats(src_ap, src_res, sl)
            P.op("act", lambda e: e.activation(out=xnbs[xb][:], in_=src_ap, func=ACTF.Identity, bias=nb[:, sl:sl + 1],
                                               scale=rstd[:, sl:sl + 1]),
                 reads=src_res + ["rstd" + t, "nb" + t], writes=["xnb%d" % xb])

        def norm_B(dstT, dst_res, col0, is_sample, m_sc, m_sh, xb=0):
            xnb = xnbs[xb]
            modr = "modT_a" if m_sc < 32 else "modT_b"
            for hf in range(2):
                b = nextbank()
                pv = psb[b][:].bitcast(BF16).rearrange("p (k t) -> p k t", k=8)
                for k in range(8):
                    kc = hf * 8 + k
                    P.op("pe", lambda e, k=k, kc=kc, pv=pv: e.transpose(
                        out=pv[:, k, :], in_=xnb[:, kc * 128:(kc + 1) * 128], identity=identb[:]),
                        reads=["xnb%d" % xb, "identb"], writes=["ps%d" % b], inc=(k == 7))
                dst = dstT[:, hf * 8:(hf + 1) * 8, col0:col0 + 128]
                if is_sample:
                    sc_ap = modT[:, m_sc + hf * 8:m_sc + hf * 8 + 8, 1:17].unsqueeze(3).to_broadcast([128, 8, 16, 8])
                    sh_ap = modT[:, m_sh + hf * 8:m_sh + hf * 8 + 8, 1:17].unsqueeze(3).to_broadcast([128, 8, 16, 8])
                    dv = dst.rearrange("p k (n s) -> p k n s", s=8)
                    pvv = pv.rearrange("p k (n s) -> p k n s", s=8)
                else:
                    for k in range(8):
                        kc = hf * 8 + k
                        o_ap = dstT[:, kc, col0:col0 + 128]
                        i_ap = pv[:, k, :]
                        sc1 = modT[:, m_sc + kc, 0:1]
                        sh1 = modT[:, m_sh + kc, 0:1]
                        if k % 4 == 0:
                            P.op("act", lambda e, o_ap=o_ap, i_ap=i_ap, sc1=sc1, sh1=sh1: e.activation(
                                out=o_ap, in_=i_ap, func=ACTF.Identity, bias=sh1, scale=sc1),
                                reads=["ps%d" % b, modr], writes=[dst_res])
                        else:
                            P.op("dve", lambda e, o_ap=o_ap, i_ap=i_ap, sc1=sc1, sh1=sh1: e.tensor_scalar(
                                out=o_ap, in0=i_ap, scalar1=sc1, scalar2=sh1, op0=ALU.mult, op1=ALU.add),
                                reads=["ps%d" % b, modr], writes=[dst_res])
                    continue
                P.op("dve", lambda e, dv=dv, pvv=pvv, sc_ap=sc_ap: e.tensor_tensor(out=dv, in0=pvv, in1=sc_ap, op=ALU.mult),
                     reads=["ps%d" % b, modr], writes=[dst_res])
                P.op("dve", lambda e, dv=dv, sh_ap=sh_ap: e.tensor_tensor(out=dv, in0=dv, in1=sh_ap, op=ALU.add),
                     reads=[dst_res, modr], writes=[dst_res])

        P.dma("sp", lambda e: e.dma_start(out=cTs[:], in_=cT_d), writes=["cTs"])
        P.dma("sp", lambda e: e.dma_start(out=badaT[:], in_=b_adaT), writes=["badaT"])
        for v in range(2):
            P.dma("sp", lambda e, v=v: e.dma_start(out=bG[:, v, :], in_=b_adaG[v:v + 1, :].partition_broadcast(17)),
                  writes=["bG"])
        P.op("act", lambda e: e.activation(out=siluT[:], in_=cTs[:], func=ACTF.Silu), reads=["cTs"], writes=["siluT"])

        def ada_epi(i, m, b):
            P.op("act", lambda e, m=m, b=b: e.activation(out=modT[:, m, :], in_=psb[b][:, 0:17], func=ACTF.Identity,
                                                          bias=badaT[:, m:m + 1]),
                 reads=["ps%d" % b, "badaT"], writes=["modT_a" if m < 32 else "modT_b"])

        tiles_all = [("h", 0)] + [("p", i) for i in range(NPT)] + [("s", 0)]
        blocks = [list(range(i, min(i + NT, NTILES))) for i in range(0, NTILES, NT)]
        ln0_done = set()

        def ln0_A(g):
            P.dma("sp", lambda e, g=g: e.dma_start(out=xs[:], in_=x_all[g]), writes=["xs"])
            norm_A(xs[:], ["xs"], sl=NT)

        def ln0_B(g, j):
            norm_B(hT, "hT%d" % j, j * 128, tiles_all[g][0] == "s", 16, 0)

        def ada_hook(i):
            b0 = blocks[0]
            if i % 8 == 3:
                j = i // 8
                if j < len(b0):
                    ln0_B(b0[j], j)
                if j + 1 < len(b0):
                    ln0_A(b0[j + 1])

        for part, m0 in ((0, 16), (1, 48)):
            stageB(w_adaB, list(range(part * 32, part * 32 + 32)), lambda kc: siluT[:, kc, :], ["siluT"],
                   lambda ps: ps[:, 0:17], ada_epi, post_hook=(ada_hook if part == 1 else None))
            mr = "modT_a" if part == 0 else "modT_b"
            P.op("dve", lambda e, m0=m0: e.tensor_scalar(out=modT[:, m0:m0 + 16, :], in0=modT[:, m0:m0 + 16, :],
                                                         scalar1=1.0, scalar2=None, op0=ALU.add),
                 reads=[mr], writes=[mr])
            if part == 0:
                ln0_A(blocks[0][0])
                ln0_done.add(0)
        for v in range(2):
            def g_epi(j, cq, b, v=v):
                P.op("dve", lambda e: e.tensor_tensor(out=gs[:, v, cq * 512:(cq + 1) * 512], in0=psb[b][0:17, :],
                                                      in1=bG[:, v, cq * 512:(cq + 1) * 512], op=ALU.add),
                     reads=["ps%d" % b, "bG"], writes=["gs%d" % v])
            stageA(w_adaG[v], 0, 16, [0], lambda kc, j: siluT[:, kc, :], lambda kc, j: ["siluT"], g_epi,
                   out_fn=lambda ps: ps[0:17, :])
            if v == 1:
                P.op("dve", lambda e: e.tensor_scalar(out=gs[:, 1, :], in0=gs[:, 1, :], scalar1=1.0 / ALPHA, scalar2=None,
                                                      op0=ALU.mult), reads=["gs1"], writes=["gs1"])
            P.dma("sp", lambda e, v=v: e.dma_start(out=gscr[:, v, :], in_=gs[:, v, :]), reads=["gs%d" % v],
                  writes=["gscr"])

        QSCALE = 0.125

        def load_bc(slot, src_row_ap, res):
            P.dma("sp", lambda e: e.dma_start(out=bc[:, slot, :], in_=src_row_ap.partition_broadcast(128)),
                  reads=["gscr"], writes=[res])

        def load_bc_sample(slot, v, res):
            for n in range(16):
                P.dma("sp", lambda e, n=n: e.dma_start(
                    out=bc[n * 8:(n + 1) * 8, slot, :], in_=gscr[1 + n:2 + n, v, :].partition_broadcast(8)),
                    reads=["gscr"], writes=[res])

        def do_block(bi, blk):
            nb_t = len(blk)
            kinds = [tiles_all[g][0] for g in blk]
            main = [j for j in range(nb_t) if kinds[j] != "h"]
            prm = [j for j in range(nb_t) if kinds[j] == "p"]
            seq = [j for j in range(nb_t) if kinds[j] in ("h", "p")]
            smp = [j for j in range(nb_t) if kinds[j] == "s"]
            Cb = nb_t * 128
            Cp = len(seq) * 128
            c_main0 = main[0] * 128
            has_sample = len(smp) > 0
            last_prompt_j = None
            for j in prm:
                if tiles_all[blk[j]][1] == NPT - 1:
                    last_prompt_j = j

            if bi not in ln0_done:
                for j in range(nb_t):
                    ln0_A(blk[j])
                    ln0_B(blk[j], j)
            hres = ["hT%d" % j for j in range(nb_t)]

            def kv_epi(j, cq, b):
                P.op("act", lambda e: e.activation(out=Vb[:, j + 1, :], in_=psb[b][:, 256:512], func=ACTF.Copy),
                     reads=["ps%d" % b], writes=["V%d" % (j + 1)])
                if j == last_prompt_j or kinds[j] == "s":
                    P.op("act", lambda e: e.activation(out=kvst[:], in_=psb[b][:, :], func=ACTF.Copy),
                         reads=["ps%d" % b], writes=["kvst"])
                    if kinds[j] == "p":
                        P.dma("sp", lambda e: e.dma_start(out=kv_last, in_=kvst[:]), reads=["kvst"], writes=["kv_last"])
                    else:
                        for n in range(16):
                            P.dma("sp", lambda e, n=n: e.dma_start(out=ks_out[n, 120:128, :],
                                                                    in_=kvst[n * 8:(n + 1) * 8, 0:256]),
                                  reads=["kvst"], writes=["ks_out"])
                            P.dma("sp", lambda e, n=n: e.dma_start(out=vs_out[n, 120:128, :],
                                                                    in_=kvst[n * 8:(n + 1) * 8, 256:512]),
                                  reads=["kvst"], writes=["vs_out"])

            stageA(w_kv, 0, 16, list(range(nb_t)), lambda kc, j: hT[:, kc, j * 128:(j + 1) * 128],
                   lambda kc, j: ["hT%d" % j], kv_epi, ncq=1)
            if has_sample:
                P.dma("sp", lambda e: e.dma_start(out=ks_out[:, 0:120, :], in_=cache_k[:, 8:128, :]), writes=["ks_out2"])
                P.dma("sp", lambda e: e.dma_start(out=vs_out[:, 0:120, :], in_=cache_v[:, 8:128, :]), writes=["vs_out2"])

            def win_epi(i, m, b):
                if m < 8:
                    P.op("act", lambda e: e.activation(out=qT[:, m, 0:Cb], in_=psb[b][:, 0:Cb], func=ACTF.Copy,
                                                       scale=QSCALE),
                         reads=["ps%d" % b], writes=["qT%d" % m])
                elif m < 10:
                    P.op("act", lambda e: e.activation(out=kT[:, m - 8, 128:128 + Cb], in_=psb[b][:, 0:Cb],
                                                       func=ACTF.Copy),
                         reads=["ps%d" % b], writes=["kT%d" % (m - 8)])
                else:
                    c, r = divmod(m - 10, 3)
                    if r == 0:
                        P.op("act", lambda e: e.activation(out=gcs[:, 0:Cb], in_=psb[b][:, 0:Cb], func=ACTF.Copy),
                             reads=["ps%d" % b], writes=["gcs"])
                    elif r == 1:
                        if Cp > 0:
                            P.op("dve", lambda e: e.tensor_copy(out=ubuf[:, 0:2], in_=carry[:, c, :]),
                                 reads=["carry"], writes=["ubuf"])
                            P.op("dve", lambda e: e.tensor_tensor(out=ubuf[:, 2:2 + Cp], in0=psb[b][:, 0:Cp],
                                                                  in1=gcs[:, 0:Cp], op=ALU.mult),
                                 reads=["ps%d" % b, "gcs", "ubuf"], writes=["ubuf"])
                            if kinds[0] == "h":
                                P.op("dve", lambda e: e.tensor_scalar(out=ubuf[:, 128:130], in0=ubuf[:, 128:130],
                                                                      scalar1=flag[:, 0:1], scalar2=None, op0=ALU.mult),
                                     reads=["ubuf", "flag"], writes=["ubuf"])
                            P.op("dve", lambda e: e.tensor_copy(out=carry[:, c, :], in_=ubuf[:, Cp:Cp + 2]),
                                 reads=["ubuf"], writes=["carry"])
                            if last_prompt_j is not None:
                                P.op("dve", lambda e: e.tensor_copy(out=ulast_p[:, c, :], in_=ubuf[:, Cp:Cp + 2]),
                                     reads=["ubuf"], writes=["ulast_p"])
                        if has_sample:
                            P.dma("sp", lambda e: e.dma_start(out=us[:, :, 0:2], in_=state_d[:, c]), writes=["us"])
                            P.op("dve", lambda e: e.tensor_tensor(
                                out=us[:, :, 2:10], in0=psb[b][:, Cp:Cb].rearrange("p (n s) -> p n s", s=8),
                                in1=gcs[:, Cp:Cb].rearrange("p (n s) -> p n s", s=8), op=ALU.mult),
                                reads=["ps%d" % b, "gcs", "us"], writes=["us"])
                            P.op("dve", lambda e: e.tensor_copy(out=ulast_s[:, c], in_=us[:, :, 8:10]),
                                 reads=["us"], writes=["ulast_s"])
                    else:
                        if Cp > 0:
                            P.op("dve", lambda e: e.tensor_scalar(out=tcv[:, 0:Cp], in0=ubuf[:, 0:Cp],
                                                                   scalar1=convw[:, c, 0:1], scalar2=None, op0=ALU.mult),
                                 reads=["ubuf", "convw"], writes=["tcv"])
                            for tap in (1, 2):
                                P.op("dve", lambda e, tap=tap: e.scalar_tensor_tensor(
                                    out=tcv[:, 0:Cp], in0=ubuf[:, tap:tap + Cp], scalar=convw[:, c, tap:tap + 1],
                                    in1=tcv[:, 0:Cp], op0=ALU.mult, op1=ALU.add),
                                    reads=["ubuf", "convw", "tcv"], writes=["tcv"])
                            P.op("dve", lambda e: e.tensor_tensor(out=mixT[:, 8 + c, 0:Cp], in0=psb[b][:, 0:Cp],
                                                                  in1=tcv[:, 0:Cp], op=ALU.mult),
                                 reads=["ps%d" % b, "tcv"], writes=["mix%d" % (8 + c)])
                        if has_sample:
                            P.op("dve", lambda e: e.tensor_scalar(out=ts[:], in0=us[:, :, 0:8],
                                                                   scalar1=convw[:, c, 0:1], scalar2=None, op0=ALU.mult),
                                 reads=["us", "convw"], writes=["ts"])
                            for tap in (1, 2):
                                P.op("dve", lambda e, tap=tap: e.scalar_tensor_tensor(
                                    out=ts[:], in0=us[:, :, tap:tap + 8], scalar=convw[:, c, tap:tap + 1],
                                    in1=ts[:], op0=ALU.mult, op1=ALU.add),
                                    reads=["us", "convw", "ts"], writes=["ts"])
                            P.op("dve", lambda e: e.tensor_tensor(
                                out=mixT[:, 8 + c, Cp:Cb].rearrange("p (n s) -> p n s", s=8),
                                in0=psb[b][:, Cp:Cb].rearrange("p (n s) -> p n s", s=8), in1=ts[:], op=ALU.mult),
                                reads=["ps%d" % b, "ts"], writes=["mix%d" % (8 + c)])

            def attn_front(ui):
                j, q = units[ui]
                par = ui % 2
                p0 = par * 4
                tri = ui % 4
                e0 = tri * 4
                gp = tiles_all[blk[j]][1]
                Sr, Per = "S%d" % par, "Pe%d" % tri
                P.op("dve", lambda e: e.tensor_copy(out=S[:, p0:p0 + 4, 256:257],
                                                    in_=sink_bc[:, 4 * q:4 * q + 4].unsqueeze(2)),
                     reads=["sink_bc"], writes=[Sr])
                for pr in range(2):
                    b = nextbank()
                    for hh in range(2):
                        h = 4 * q + pr * 2 + hh
                        qc, off, kch = head_loc(h)
                        P.op("pe", lambda e, b=b, hh=hh, qc=qc, off=off, kch=kch: e.matmul(
                            psb[b][:, hh * 256:(hh + 1) * 256],
                            lhsT=qT[off:off + 64, qc, j * 128:(j + 1) * 128],
                            rhs=kT[off:off + 64, kch, j * 128:j * 128 + 256], start=True, stop=True),
                            reads=["qT%d" % qc, "kT%d" % kch, "kTp"], writes=["ps%d" % b], inc=(hh == 1))
                    h0 = 4 * q + pr * 2
                    P.op("dve", lambda e, b=b, pr=pr, h0=h0: e.tensor_tensor(
                        out=S[:, p0 + pr * 2:p0 + pr * 2 + 2, 0:256], in0=psb[b][:].rearrange("p (a k) -> p a k", a=2),
                        in1=bias_t[:, h0:h0 + 2, :], op=ALU.add),
                        reads=["ps%d" % b, "bias_t"], writes=[Sr])
                if gp == 0:
                    P.op("dve", lambda e: e.tensor_tensor(
                        out=S[:, p0:p0 + 4, 0:128], in0=S[:, p0:p0 + 4, 0:128],
                        in1=prevmask[:].unsqueeze(1).to_broadcast([128, 4, 128]), op=ALU.add),
                        reads=[Sr, "prevmask"], writes=[Sr])
                P.op("dve", lambda e: e.reduce_max(out=mx[:, e0:e0 + 4], in_=S[:, p0:p0 + 4, 0:257], axis=AX.X),
                     reads=[Sr], writes=["mx%d" % tri])
                P.op("dve", lambda e: e.tensor_scalar(out=negm[:, e0:e0 + 4], in0=mx[:, e0:e0 + 4], scalar1=-1.0,
                                                      scalar2=None, op0=ALU.mult),
                     reads=["mx%d" % tri], writes=["negm%d" % tri])
                for hl in range(4):
                    P.op("act", lambda e, hl=hl: e.activation(
                        out=Pe[:, e0 + hl, 0:257], in_=S[:, p0 + hl, 0:257], func=ACTF.Exp,
                        bias=negm[:, e0 + hl:e0 + hl + 1], accum_out=sm[:, e0 + hl:e0 + hl + 1]),
                        reads=[Sr, "negm%d" % tri], writes=[Per, "sm%d" % tri])
                P.op("dve", lambda e: e.reciprocal(out=rc[:, e0:e0 + 4], in_=sm[:, e0:e0 + 4]),
                     reads=["sm%d" % tri], writes=["rc%d" % tri])

            PTs = [PT, PT2]

            def attn_back1(ui):
                j, q = units[ui]
                tri = ui % 4
                e0 = tri * 4
                Per = "Pe%d" % tri
                ptb = ui % 2
                PTr = "PT" if ptb == 0 else "PT2"
                b = nextbank()
                pv = psb[b][:].bitcast(BF16).rearrange("p (a c t) -> p a c t", a=4, c=2)
                for a_ in range(4):
                    for c2 in range(2):
                        P.op("pe", lambda e, pv=pv, a_=a_, c2=c2: e.transpose(
                            out=pv[:, a_, c2, :], in_=Pe[:, e0 + a_, c2 * 128:(c2 + 1) * 128], identity=identb[:]),
                            reads=[Per, "identb"], writes=["ps%d" % b], inc=(a_ == 3 and c2 == 1))
                P.op("act", lambda e, pv=pv: e.activation(out=PTs[ptb][:, 0:4], in_=pv, func=ACTF.Copy),
                     reads=["ps%d" % b], writes=[PTr])

            def attn_back2(ui):
                j, q = units[ui]
                tri = ui % 4
                e0 = tri * 4
                ptb = ui % 2
                PTr = "PT" if ptb == 0 else "PT2"
                b2 = nextbank()
                for hl in range(4):
                    for c2 in range(2):
                        P.op("pe", lambda e, b2=b2, hl=hl, c2=c2: e.matmul(
                            psb[b2][:, hl * 64:(hl + 1) * 64], lhsT=PTs[ptb][:, hl, c2, :],
                            rhs=Vb[:, j + c2, q * 64:(q + 1) * 64], start=(c2 == 0), stop=(c2 == 1)),
                            reads=[PTr, "V%d" % (j + c2)], writes=["ps%d" % b2], inc=(hl == 3 and c2 == 1))
                P.op("dve", lambda e, b2=b2: e.tensor_tensor(
                    out=Abuf[:, q * 256:(q + 1) * 256].rearrange("p (a d) -> p a d", d=64),
                    in0=psb[b2][:, 0:256].rearrange("p (a d) -> p a d", d=64),
                    in1=rc[:, e0:e0 + 4].unsqueeze(2).to_broadcast([128, 4, 64]), op=ALU.mult),
                    reads=["ps%d" % b2, "rc%d" % tri], writes=["Abuf"])

            def attn_back3(ui):
                j, q = units[ui]
                if q == 3:
                    b3 = nextbank()
                    pv3 = psb[b3][:].bitcast(BF16).rearrange("p (k t) -> p k t", k=8)
                    for k in range(8):
                        P.op("pe", lambda e, pv3=pv3, k=k: e.transpose(out=pv3[:, k, :], in_=Abuf[:, k * 128:(k + 1) * 128],
                                                                       identity=identb[:]),
                             reads=["Abuf", "identb"], writes=["ps%d" % b3], inc=(k == 7))
                    P.op("act", lambda e, pv3=pv3: e.activation(out=mixT[:, 0:8, j * 128:(j + 1) * 128], in_=pv3,
                                                                func=ACTF.Copy),
                         reads=["ps%d" % b3], writes=["mixA%d" % j])

            units = [(j, q) for j in prm for q in range(4)]
            apieces = []
            nu = len(units)
            for step in range(nu + 4):
                if step < nu:
                    apieces.append(lambda ui=step: attn_front(ui))
                if 0 <= step - 2 < nu:
                    apieces.append(lambda ui=step - 2: attn_back1(ui))
                if 0 <= step - 4 < nu and units[step - 4][1] == 3:
                    apieces.append(lambda ui=step - 4: attn_back3(ui))
                if 0 <= step - 3 < nu:
                    apieces.append(lambda ui=step - 3: attn_back2(ui))
            n_ap = len(apieces)
            emitted = [0]

            def win_hook(i):
                if i < 9:
                    return
                target = (n_ap * (i - 8) + 24) // 25
                while emitted[0] < target and apieces:
                    apieces.pop(0)()
                    emitted[0] += 1

            stageB(w_inB, list(range(34)), lambda kc: hT[:, kc, 0:Cb], hres, lambda ps: ps[:, 0:Cb], win_epi,
                   post_hook=win_hook)

            if last_prompt_j is not None:
                b = nextbank()
                b2 = nextbank()
                for c in range(8):
                    bb = b if c < 4 else b2
                    P.op("pe", lambda e, c=c, bb=bb: e.transpose(out=psb[bb][0:2, (c % 4) * 128:(c % 4 + 1) * 128],
                                                                   in_=ulast_p[:, c, :], identity=identf[:]),
                         reads=["ulast_p", "identf"], writes=["ps%d" % bb], inc=(c % 4 == 3))
                for hf, bb in enumerate((b, b2)):
                    P.op("act", lambda e, hf=hf, bb=bb: e.activation(out=cvst[0:2, hf * 512:(hf + 1) * 512],
                                                                     in_=psb[bb][0:2, :], func=ACTF.Copy),
                         reads=["ps%d" % bb], writes=["cvst"])
                P.dma("sp", lambda e: e.dma_start(out=conv_last, in_=cvst[0:2, :]), reads=["cvst"], writes=["conv_last"])
            if has_sample:
                b = nextbank()
                b2 = nextbank()
                for c in range(8):
                    bb = b if c < 4 else b2
                    P.op("pe", lambda e, c=c, bb=bb: e.transpose(
                        out=psb[bb][0:32, (c % 4) * 128:(c % 4 + 1) * 128],
                        in_=ulast_s[:, c].rearrange("p n r -> p (n r)"), identity=identf[:]),
                        reads=["ulast_s", "identf"], writes=["ps%d" % bb], inc=(c % 4 == 3))
                for hf, bb in enumerate((b, b2)):
                    P.op("act", lambda e, hf=hf, bb=bb: e.activation(out=cvst[0:32, hf * 512:(hf + 1) * 512],
                                                                     in_=psb[bb][0:32, :], func=ACTF.Copy),
                         reads=["ps%d" % bb, "conv_last"], writes=["cvst"])
                P.dma("sp", lambda e: e.dma_start(out=convs_out, in_=cvst[0:32, :]), reads=["cvst"], writes=["convs_out"])

            while apieces:
                apieces.pop(0)()

            if len(seq) > 0 and bi + 1 < len(blocks):
                lj = seq[-1]
                P.op("dve", lambda e, lj=lj: e.tensor_copy(out=kT[:, :, 0:128], in_=kT[:, :, 128 + lj * 128:256 + lj * 128]),
                     reads=["kT0", "kT1", "kTp"], writes=["kTp"])
                P.op("dve", lambda e, lj=lj: e.tensor_copy(out=Vb[:, 0, :], in_=Vb[:, lj + 1, :]),
                     reads=["V%d" % (lj + 1), "V0"], writes=["V0"])

            if has_sample:
                js = smp[0]
                cs0 = js * 128
                P.dma("pool", lambda e: e.dma_start(out=selb[:], in_=sel_d), writes=["selb"])
                for grp in range(2):
                    P.op("dve", lambda e, grp=grp: e.tensor_copy(
                        out=qs[:, grp].rearrange("p n (g s) -> p n g s", s=8),
                        in_=qT[:, grp * 4:grp * 4 + 4, cs0:cs0 + 128].rearrange("p g (n s) -> p n g s", s=8)),
                        reads=["qT%d" % (grp * 4 + i) for i in range(4)], writes=["qs"])
                for g in range(2):
                    n0 = g * 8
                    P.dma("pool", lambda e, n0=n0: e.dma_start(out=ck[:], in_=cache_k[n0:n0 + 8].rearrange("n k d -> k n d")),
                          writes=["ck"])
                    P.dma("pool", lambda e, n0=n0: e.dma_start(out=cv[:], in_=cache_v[n0:n0 + 8].rearrange("n k d -> k n d")),
                          writes=["cv"])
                    for n in range(8):
                        r0 = cs0 - 0 + 0
                        P.dma("sp", lambda e, n=n, n0=n0: e.dma_start(out=vnew[0:8, n, :],
                                                                       in_=Vb[(n0 + n) * 8:(n0 + n + 1) * 8, js + 1, :]),
                              reads=["V%d" % (js + 1)], writes=["vnew"])
                    for q4 in range(2):
                        b = nextbank()
                        pv = psb[b][:].bitcast(BF16).rearrange("p (a c t) -> p a c t", a=4, c=2)
                        for a in range(4):
                            for c2 in range(2):
                                P.op("pe", lambda e, pv=pv, a=a, c2=c2, q4=q4: e.transpose(
                                    out=pv[:, a, c2, :], in_=ck[:, q4 * 4 + a, c2 * 128:(c2 + 1) * 128], identity=identb[:]),
                                    reads=["ck", "identb"], writes=["ps%d" % b], inc=(a == 3 and c2 == 1))
                        P.op("act", lambda e, pv=pv, q4=q4: e.activation(out=KT[:, q4 * 4:q4 * 4 + 4, :, 0:128], in_=pv,
                                                                         func=ACTF.Copy),
                             reads=["ps%d" % b], writes=["KT"])
                    for c2 in range(2):
                        P.op("dve", lambda e, c2=c2, n0=n0: e.tensor_copy(
                            out=KT[:, :, c2, 128:136],
                            in_=kT[:, c2, 128 + cs0 + n0 * 8:128 + cs0 + n0 * 8 + 64].rearrange("p (n s) -> p n s", s=8)),
                            reads=["kT%d" % c2, "KT"], writes=["KT"])
                    sbk = [nextbank() for _ in range(3)]
                    for n in range(8):
                        b = sbk[n // 3]
                        for jj in range(4):
                            off = 64 * (jj % 2)
                            cb0 = 0 if jj < 2 else 4
                            col = cs0 + (n0 + n) * 8
                            P.op("pe", lambda e, b=b, n=n, jj=jj, off=off, cb0=cb0, col=col, n0=n0: e.matmul(
                                psb[b][32 * jj:32 * jj + 32, (n % 3) * 136:(n % 3) * 136 + 136],
                                lhsT=qs[off:off + 64, cb0 // 4, n0 + n, :],
                                rhs=KT[off:off + 64, n, jj // 2, :], start=True, stop=True, tile_position=(off, 32 * jj)),
                                reads=["KT", "qs"], writes=["ps%d" % b],
                                inc=(jj == 3 and (n % 3 == 2 or n == 7)))
                    for gi in range(3):
                        nn = 3 if gi < 2 else 2
                        b = sbk[gi]
                        P.op("dve", lambda e, b=b, gi=gi, nn=nn: e.tensor_tensor(
                            out=Ss[:, gi * 3:gi * 3 + nn, 0:136],
                            in0=psb[b][:, 0:nn * 136].rearrange("p (n k) -> p n k", k=136),
                            in1=sbias[:].unsqueeze(1).to_broadcast([128, nn, 136]), op=ALU.add),
                            reads=["ps%d" % b, "sbias"], writes=["Ss"])
                    P.op("dve", lambda e: e.tensor_copy(out=Ss[:, :, 136:137],
                                                        in_=sink_s[:, 0:1].unsqueeze(1).to_broadcast([128, 8, 1])),
                         reads=["sink_s"], writes=["Ss"])
                    P.op("dve", lambda e: e.reduce_max(out=mx[:, 0:8], in_=Ss[:, :, 0:137], axis=AX.X), reads=["Ss"], writes=["mx0", "mx1"])
                    P.op("dve", lambda e: e.tensor_scalar(out=negm[:, 0:8], in0=mx[:, 0:8], scalar1=-1.0, scalar2=None, op0=ALU.mult),
                         reads=["mx0", "mx1"], writes=["negm0", "negm1"])
                    for n in range(8):
                        P.op("act", lambda e, n=n: e.activation(out=Pf[:, n, 0:137], in_=Ss[:, n, 0:137], func=ACTF.Exp,
                                                                bias=negm[:, n:n + 1], accum_out=sm[:, n:n + 1]),
                             reads=["Ss", "negm0", "negm1"], writes=["Pf", "sm0", "sm1"])
                    P.op("dve", lambda e: e.reciprocal(out=rc[:, 0:8], in_=sm[:, 0:8]), reads=["sm0", "sm1"], writes=["rc0", "rc1"])
                    P.op("dve", lambda e: e.tensor_tensor(out=Pb[:], in0=Pf[:, :, 0:136],
                                                          in1=rc[:, 0:8].unsqueeze(2).to_broadcast([128, 8, 136]), op=ALU.mult),
                         reads=["Pf", "rc0", "rc1"], writes=["Pb"])
                    b1 = nextbank()
                    b2 = nextbank()
                    pv1 = psb[b1][:].bitcast(BF16).rearrange("p (n t) -> p n t", n=8)
                    pv2 = psb[b2][:].bitcast(BF16).rearrange("p (n t) -> p n t", n=8)
                    for n in range(8):
                        P.op("pe", lambda e, n=n, pv1=pv1: e.transpose(out=pv1[:, n, :], in_=Pb[:, n, 0:128], identity=identb[:]),
                             reads=["Pb", "identb"], writes=["ps%d" % b1], inc=(n == 7))
                    for n in range(8):
                        P.op("pe", lambda e, n=n, pv2=pv2: e.transpose(out=pv2[0:8, n, :], in_=Pb[:, n, 128:136], identity=identb[:]),
                             reads=["Pb", "identb"], writes=["ps%d" % b2], inc=(n == 7))
                    P.op("act", lambda e, pv1=pv1: e.activation(out=PTc[:], in_=pv1, func=ACTF.Copy),
                         reads=["ps%d" % b1], writes=["PTc"])
                    P.op("act", lambda e, pv2=pv2: e.activation(out=PTn[0:8], in_=pv2[0:8], func=ACTF.Copy),
                         reads=["ps%d" % b2], writes=["PTn"])
                    bo_ = nextbank()
                    for n in range(8):
                        for jj in range(4):
                            P.op("pe", lambda e, n=n, jj=jj, bo_=bo_: e.matmul(
                                psb[bo_][32 * jj:32 * jj + 32, n * 64:(n + 1) * 64],
                                lhsT=PTc[:, n, 32 * jj:32 * jj + 32], rhs=cv[:, n, jj * 64:(jj + 1) * 64],
                                start=True, stop=False, tile_position=(0, 32 * jj)),
                                reads=["cv", "PTc"], writes=["ps%d" % bo_], inc=False)
                            P.op("pe", lambda e, n=n, jj=jj, bo_=bo_: e.matmul(
                                psb[bo_][32 * jj:32 * jj + 32, n * 64:(n + 1) * 64],
                                lhsT=PTn[0:8, n, 32 * jj:32 * jj + 32], rhs=vnew[0:8, n, jj * 64:(jj + 1) * 64],
                                start=False, stop=True, tile_position=(0, 32 * jj)),
                                reads=["vnew", "PTn"], writes=["ps%d" % bo_], inc=(n == 7 and jj == 3))
                    P.op("act", lambda e, bo_=bo_: e.activation(out=osb[:], in_=psb[bo_][:].rearrange("p (n d) -> p n d", d=64),
                                                       func=ACTF.Copy),
                         reads=["ps%d" % bo_], writes=["osb"])
                    by_ = nextbank()
                    for n in range(8):
                        for par in range(2):
                            P.op("pe", lambda e, n=n, par=par, by_=by_: e.matmul(
                                psb[by_][64 * par:64 * par + 64, n * 64:(n + 1) * 64],
                                lhsT=osb[:, n, :], rhs=selb[:, par, :], start=True, stop=True,
                                tile_position=(0, 64 * par)),
                                reads=["osb", "selb"], writes=["ps%d" % by_], inc=(n == 7 and par == 1))
                    P.op("act", lambda e, n0=n0, by_=by_: e.activation(
                        out=mixT[:, 0:8, cs0 + n0 * 8:cs0 + n0 * 8 + 64].rearrange("p c (n s) -> p n c s", s=8),
                        in_=psb[by_][:].rearrange("p (n c s) -> p n c s", c=8, s=8), func=ACTF.Copy),
                        reads=["ps%d" % by_], writes=["mixA%d" % js])

            load_bc(0, gscr[0:1, 0, :], "bc0")
            if has_sample:
                load_bc_sample(1, 0, "bc1")
            load_bc(2, lnp_d[0:1, :], "bc2")
            load_bc(3, lnp_d[1:2, :], "bc3")

            def wo_epi(j, cq, b):
                slot = 1 if kinds[j] == "s" else 0
                P.op("dve", lambda e: e.tensor_tensor(out=acc[:, j, cq * 512:(cq + 1) * 512], in0=psb[b][:, :],
                                                      in1=bc[:, slot, cq * 512:(cq + 1) * 512], op=ALU.mult),
                     reads=["ps%d" % b, "bc%d" % slot], writes=["acc%d" % j])

            if has_sample:
                xsl, xres = [xs[:], xs[:]], ["xs", "xs"]
            else:
                xsl, xres = [xs[:], bc[:, 1, :]], ["xs", "bc1"]
            groups = [main]
            xcount = [0]
            xslot = {}

            def xload(j):
                k = xcount[0] % 2
                xcount[0] += 1
                xslot[j] = k
                P.dma("sp", lambda e, g=blk[j], xa=xsl[k]: e.dma_start(out=xa, in_=x_all[g]), writes=[xres[k]])

            def chain(tl):
                for j in tl:
                    ar = ["acc%d" % j]
                    if j not in xslot:
                        xload(j)
                    xa, xr = xsl[xslot[j]], xres[xslot[j]]
                    P.op("dve", lambda e, j=j, xa=xa: e.scalar_tensor_tensor(out=acc[:, j, :], in0=xa, scalar=ALPHA,
                                                                             in1=acc[:, j, :], op0=ALU.mult, op1=ALU.add),
                         reads=[xr] + ar, writes=ar)
                ln_stats_multi(tl, lambda j: acc[:, j, :], lambda j: ["acc%d" % j])
                for j in tl:
                    ar = ["acc%d" % j]
                    P.op("act", lambda e, j=j: e.activation(out=acc[:, j, :], in_=acc[:, j, :], func=ACTF.Identity,
                                                            bias=nb[:, j:j + 1], scale=rstd[:, j:j + 1]),
                         reads=ar + ["rstd_%d" % j, "nb_%d" % j], writes=ar)
                for j in tl:
                    ar = ["acc%d" % j]
                    P.op("dve", lambda e, j=j: e.tensor_tensor(out=acc[:, j, :], in0=acc[:, j, :], in1=bc[:, 2, :], op=ALU.mult),
                         reads=ar + ["bc2"], writes=ar)
                    P.op("dve", lambda e, j=j: e.tensor_tensor(out=acc[:, j, :], in0=acc[:, j, :], in1=bc[:, 3, :], op=ALU.add),
                         reads=ar + ["bc3"], writes=ar)
                ln_stats_multi(tl, lambda j: acc[:, j, :], lambda j: ["acc%d" % j])
                def cA(ti):
                    j = tl[ti]
                    norm_A(acc[:, j, :], ["acc%d" % j], sl=j, xb=ti % 2, stats=False)

                def cB(ti):
                    j = tl[ti]
                    norm_B(hT, "hT%d" % j, j * 128, kinds[j] == "s", 48, 32, xb=ti % 2)

                for ti in range(len(tl)):
                    cA(ti)
                    if ti >= 1:
                        cB(ti - 1)
                cB(len(tl) - 1)

            if not has_sample:
                for j in groups[0][:2]:
                    xload(j)
            for gi, grp in enumerate(groups):
                stageA(w_o, 0, 16, grp, lambda kc, j: mixT[:, kc, j * 128:(j + 1) * 128],
                       lambda kc, j: (["mixA%d" % j] if kc < 8 else ["mix%d" % kc]), wo_epi)
                if has_sample:
                    for j in grp:
                        xload(j)
                        chain([j])
                else:
                    chain(grp)
                    if gi + 1 < len(groups):
                        for j in groups[gi + 1]:
                            xload(j)

            load_bc(0, gscr[0:1, 1, :], "bc0")
            if has_sample:
                load_bc_sample(1, 1, "bc1")
            load_bc(2, lnp_d[2:3, :], "bc2")
            load_bc(3, lnp_d[3:4, :], "bc3")
            Cm = Cb - c_main0
            h2res = ["hT%d" % j for j in main]
            for fh in range(2):
                def up_epi(i, m, b):
                    ml = m - fh * 32
                    P.op("act", lambda e: e.activation(out=rT[:, 0:Cm], in_=psb[b][:, 0:Cm], func=ACTF.Square),
                         reads=["ps%d" % b], writes=["rT"])
                    P.op("dve", lambda e: e.scalar_tensor_tensor(out=uT[:, ml, c_main0:Cb], in0=psb[b][:, 0:Cm], scalar=0.0,
                                                                  in1=rT[:, 0:Cm], op0=ALU.is_gt, op1=ALU.mult),
                         reads=["ps%d" % b, "rT"], writes=["uT%d" % ml])

                uphook = None
                if fh == 1 and bi + 1 < len(blocks):
                    def uphook(i, nblk0=blocks[bi + 1]):
                        if i == 24:
                            ln0_A(nblk0[0])
                stageB(w_upB, list(range(fh * 32, fh * 32 + 32)), lambda kc: hT[:, kc, c_main0:Cb], h2res,
                       lambda ps: ps[:, 0:Cm], up_epi, post_hook=uphook)

                def dn_epi(j, cq, b):
                    slot = 1 if kinds[j] == "s" else 0
                    P.op("dve", lambda e: e.tensor_tensor(out=tmpq[:], in0=psb[b][:, :],
                                                          in1=bc[:, slot, cq * 512:(cq + 1) * 512], op=ALU.mult),
                         reads=["ps%d" % b, "bc%d" % slot], writes=["tmpq"])
                    P.op("dve", lambda e: e.tensor_tensor(out=acc[:, j, cq * 512:(cq + 1) * 512],
                                                           in0=acc[:, j, cq * 512:(cq + 1) * 512], in1=tmpq[:], op=ALU.add),
                         reads=["tmpq", "acc%d" % j], writes=["acc%d" % j])

                hook = None
                if fh == 1 and bi + 1 < len(blocks):
                    nblk = blocks[bi + 1]

                    def hook(cq, nblk=nblk):
                        if cq < len(nblk):
                            ln0_B(nblk[cq], cq)
                        if cq + 1 < len(nblk):
                            ln0_A(nblk[cq + 1])
                stageA(w_down, fh * 4096, 32, main, lambda kc, j: uT[:, kc, j * 128:(j + 1) * 128],
                       lambda kc, j: ["uT%d" % kc], dn_epi, cq_hook=hook)
                if hook is not None:
                    ln0_done.add(bi + 1)
            ln_stats_multi(main, lambda j: acc[:, j, :], lambda j: ["acc%d" % j], eps_t=epsb2, eps_r="epsb2")
            for j in main:
                ar = ["acc%d" % j]
                P.op("act", lambda e, j=j: e.activation(out=acc[:, j, :], in_=acc[:, j, :], func=ACTF.Identity,
                                                        bias=nb[:, j:j + 1], scale=rstd[:, j:j + 1]),
                     reads=ar + ["rstd_%d" % j, "nb_%d" % j], writes=ar)
            for j in main:
                ar = ["acc%d" % j]
                P.op("dve", lambda e, j=j: e.tensor_tensor(out=acc[:, j, :], in0=acc[:, j, :], in1=bc[:, 2, :], op=ALU.mult),
                     reads=ar + ["bc2"], writes=ar)
                P.op("dve", lambda e, j=j: e.tensor_tensor(out=acc[:, j, :], in0=acc[:, j, :], in1=bc[:, 3, :], op=ALU.add),
                     reads=ar + ["bc3"], writes=ar)
                yi = blk[j] - 1
                P.dma("sp", lambda e, j=j, yi=yi: e.dma_start(out=y_all[yi], in_=acc[:, j, :]), reads=ar, writes=["y%d" % yi])

        for bi, blk in enumerate(blocks):
            do_block(bi, blk)
        P.barrier()

        P.run()
    return nc


def _bucket(dist):
    n = np.maximum(dist, 0)
    nf = np.maximum(n, 1).astype(np.float32)
    large = 16 + (np.log(nf / np.float32(16)) / np.float32(math.log(128 / 16)) * np.float32(16)).astype(np.int32)
    large = np.minimum(large, 31)
    return np.where(n < 16, n, large)


def shared_inputs(w_ada, b_ada, w_in, attn_sinks, rel_bias, conv_w, w_o, ln1_g, ln1_b, w_up, w_down, ln2_g, ln2_b):
    f = np.float32
    w_ada = np.asarray(w_ada, f)
    b_ada = np.asarray(b_ada, f)
    w_in = np.asarray(w_in, f)
    rel_bias = np.asarray(rel_bias, f)
    sinks = np.asarray(attn_sinks, f)

    def toB(W, cols_list):
        out = np.empty((len(cols_list), 128, 16, 128), f)
        Wr = W.reshape(16, 128, W.shape[1])
        for m, cols in enumerate(cols_list):
            out[m] = Wr[:, :, cols].transpose(1, 0, 2)
        return out

    voff = [0, 2048, 6144, 8192]
    ada_cols = [np.arange(voff[m // 16] + (m % 16) * 128, voff[m // 16] + (m % 16) * 128 + 128) for m in range(64)]
    sh = {}
    sh["w_adaB"] = toB(w_ada, ada_cols)
    sh["w_adaG"] = np.ascontiguousarray(np.stack([w_ada[:, 4096:6144], w_ada[:, 10240:12288]]))
    sh["b_adaT"] = np.ascontiguousarray(np.stack([b_ada[c] for c in ada_cols], axis=1))
    sh["b_adaG"] = np.ascontiguousarray(np.stack([b_ada[4096:6144], b_ada[10240:12288]]))
    qpairs = [(0, 4), (1, 5), (2, 6), (3, 7), (8, 12), (9, 13), (10, 14), (11, 15)]
    cols = []
    for a, b in qpairs:
        cols.append(np.concatenate([np.arange(a * 64, a * 64 + 64), np.arange(b * 64, b * 64 + 64)]))
    cols.append(np.arange(1024, 1152))
    cols.append(np.arange(1152, 1280))
    for c in range(8):
        cols.append(np.arange(2560 + c * 128, 2560 + (c + 1) * 128))
        cols.append(np.arange(3584 + c * 128, 3584 + (c + 1) * 128))
        cols.append(np.arange(1536 + c * 128, 1536 + (c + 1) * 128))
    sh["w_inB"] = toB(w_in, cols)
    sh["w_kv"] = np.ascontiguousarray(w_in[:, 1024:1536])
    sh["w_o"] = np.asarray(w_o, f)
    sh["w_upB"] = toB(np.asarray(w_up, f), [np.arange(m * 128, (m + 1) * 128) for m in range(64)])
    sh["w_down"] = np.asarray(w_down, f)
    q = np.arange(128)[:, None]
    k = np.arange(256)[None, :]
    dist = q + 128 - k
    valid = (dist >= 0) & (dist < 128)
    bt = rel_bias[_bucket(dist)]
    bt = np.where(valid[:, :, None], bt, f(NEG)).astype(f)
    sh["bias_tab"] = np.ascontiguousarray(bt.transpose(0, 2, 1))
    p = np.arange(128)
    hh = (p // 32) * 4 + (p % 32) // 8
    s = p % 8
    key = np.arange(136)[None, :]
    dist = np.where(key < 128, 128 + s[:, None] - key, s[:, None] - (key - 128))
    valid = (dist >= 0) & (dist < 128)
    sb_ = rel_bias[_bucket(dist), hh[:, None]]
    sh["sbias"] = np.where(valid, sb_, f(NEG)).astype(f)
    sh["sink_bc"] = np.ascontiguousarray(np.broadcast_to(sinks[None, :], (128, 16))).astype(f)
    sh["sink_s"] = np.ascontiguousarray(sinks[hh][:, None]).astype(f)
    sh["conv_wT"] = np.ascontiguousarray(np.asarray(conv_w, f).reshape(3, 8, 128).transpose(2, 1, 0))
    sh["ident"] = np.eye(128, dtype=f)
    sel = np.zeros((128, 2, 64), f)
    for r in range(128):
        hh_ = (r // 32) * 4 + (r % 32) // 8
        sel[r, hh_ % 2, (hh_ // 2) * 8 + (r % 8)] = 1.0
    sh["sel"] = sel
    sh["lnp"] = np.ascontiguousarray(np.stack([np.asarray(a, f) for a in (ln1_g, ln1_b, ln2_g, ln2_b)]))
    return sh


def core_inputs(xp_seq, chunk, npt, x_s, c_p, c_s, ck, cv, stc):
    f = np.float32
    x_all = np.zeros((npt + 2, 128, D), f)
    lo = chunk * npt * 128
    if chunk > 0:
        x_all[0] = xp_seq[lo - 128:lo]
    x_all[1:npt + 1] = xp_seq[lo:lo + npt * 128].reshape(npt, 128, D)
    x_all[npt + 1] = x_s.reshape(128, D)
    cc = np.concatenate([c_p[None, :], c_s], axis=0)
    d = {}
    d["x_all"] = x_all
    d["cT"] = np.ascontiguousarray(cc.reshape(17, 16, 128).transpose(2, 1, 0)).astype(f)
    d["prevmask"] = np.full((128, 128), 0.0 if chunk > 0 else NEG, f)
    d["flag"] = np.full((128, 1), 1.0 if chunk > 0 else 0.0, f)
    d["cache_k"] = np.ascontiguousarray(ck.reshape(16, 128, 256)).astype(f)
    d["cache_v"] = np.ascontiguousarray(cv.reshape(16, 128, 256)).astype(f)
    d["stateT"] = np.ascontiguousarray(stc.reshape(16, 2, 8, 128).transpose(3, 2, 0, 1)).astype(f)
    return d


_NC_CACHE = {}


def kernel(x_prompt, x_sample, cache_k, cache_v, state_conv, c_prompt, c_sample,
           w_ada, b_ada, w_in, attn_sinks, rel_bias, conv_w, w_o, ln1_g, ln1_b,
           w_up, w_down, ln2_g, ln2_b):
    f = np.float32
    x_prompt = np.asarray(x_prompt, f)
    x_sample = np.asarray(x_sample, f)
    cache_k = np.asarray(cache_k, f)
    cache_v = np.asarray(cache_v, f)
    state_conv = np.asarray(state_conv, f)
    c_prompt = np.asarray(c_prompt, f)
    c_sample = np.asarray(c_sample, f)
    B, T = x_prompt.shape[0], x_prompt.shape[1]
    NPT = T // 128 // 4
    sh = shared_inputs(w_ada, b_ada, w_in, attn_sinks, rel_bias, conv_w, w_o, ln1_g, ln1_b, w_up, w_down, ln2_g, ln2_b)
    in_maps = []
    for c in range(8):
        b, chunk = c // 4, c % 4
        d = core_inputs(x_prompt[b], chunk, NPT, x_sample[16 * c:16 * c + 16], c_prompt[b], c_sample[16 * c:16 * c + 16],
                        cache_k[16 * c:16 * c + 16], cache_v[16 * c:16 * c + 16], state_conv[16 * c:16 * c + 16])
        d.update(sh)
        in_maps.append(d)
    if NPT not in _NC_CACHE:
        _NC_CACHE[NPT] = build(NPT)
    nc = _NC_CACHE[NPT]
    res = run_bass_kernel_spmd(nc, in_maps, core_ids=list(range(8))).results
    yp = np.empty((B, T, D), f)
    ys = np.empty((128, 8, D), f)
    for c in range(8):
        b, chunk = c // 4, c % 4
        ya = res[c]["y_all"]
        yp[b, chunk * NPT * 128:(chunk + 1) * NPT * 128] = ya[:NPT].reshape(NPT * 128, D)
        ys[16 * c:16 * c + 16] = ya[NPT].reshape(16, 8, D)
    k_prompt = np.stack([res[3]["kv_last"][:, 0:256], res[7]["kv_last"][:, 0:256]]).reshape(B, 128, 4, 64)
    v_prompt = np.stack([res[3]["kv_last"][:, 256:512], res[7]["kv_last"][:, 256:512]]).reshape(B, 128, 4, 64)
    conv_prompt = np.stack([res[3]["conv_last"], res[7]["conv_last"]]).reshape(B, 2, 1024)
    k_sample = np.concatenate([res[c]["ks_out"] for c in range(8)]).reshape(128, 128, 4, 64)
    v_sample = np.concatenate([res[c]["vs_out"] for c in range(8)]).reshape(128, 128, 4, 64)
    conv_sample = np.concatenate([res[c]["convs_out"].reshape(16, 2, 1024) for c in range(8)])
    return (yp, ys, k_prompt.astype(f), v_prompt.astype(f), conv_prompt.astype(f),
            k_sample.astype(f), v_sample.astype(f), conv_sample.astype(f))
```

```python
import math
from contextlib import ExitStack

import numpy as np
import concourse.bass as bass
import concourse.mybir as mybir
from concourse.bass_utils import run_bass_kernel_spmd

F32 = mybir.dt.float32
BF16 = mybir.dt.bfloat16
ACTF = mybir.ActivationFunctionType
ALU = mybir.AluOpType
AX = mybir.AxisListType

D = 2048
DFF = 8192
NEG = -1e30
ALPHA = 2.0 ** 0.25
EPS = 1e-5
ENGS = ("pe", "act", "dve", "pool", "sp")


class Prog:
    def __init__(self, nc, stack, ndma=8):
        self.nc = nc
        self.q = {e: [] for e in ENGS}
        self.sem = {}
        self.cnt = {}
        for e in ("pe", "act", "dve", "pool"):
            self.sem[e] = stack.enter_context(nc.semaphore("s_" + e))
            self.cnt[e] = 0
        self.dsem = {}
        self.dnext = {}
        for e in ("sp", "pool"):
            self.dsem[e] = []
            for i in range(ndma):
                k = "d_%s%d" % (e, i)
                self.sem[k] = stack.enter_context(nc.semaphore(k))
                self.cnt[k] = 0
                self.dsem[e].append(k)
            self.dnext[e] = 0
        self.bar = stack.enter_context(nc.semaphore("s_bar"))
        self.barcnt = 0
        self.waited = {e: {} for e in ENGS}
        self.lastw = {}
        self.reads = {}
        self.range = {}

    def alias(self, name, lo, n):
        self.range[name] = (lo, lo + n)

    def _overl(self, w):
        if w not in self.range:
            return ()
        lo, hi = self.range[w]
        return [r for r, (a, b) in self.range.items() if r != w and a < hi and lo < b]

    def _need(self, eng, ev, out):
        if ev is None:
            return
        k, v = ev
        if k == eng and (eng == "pe" or v > self.cnt[eng]):
            return
        if self.waited[eng].get(k, 0) >= v:
            return
        if out.get(k, 0) < v:
            out[k] = v

    def _deps(self, eng, reads, writes):
        need = {}
        for r in reads:
            self._need(eng, self.lastw.get(r), need)
        for w in writes:
            for w2 in [w] + list(self._overl(w)):
                self._need(eng, self.lastw.get(w2), need)
                for ev in self.reads.get(w2, ()):
                    self._need(eng, ev, need)
        for k, v in need.items():
            self.waited[eng][k] = v
        return list(need.items())

    def _commit(self, ev, reads, writes):
        for r in reads:
            self.reads.setdefault(r, []).append(ev)
        for w in writes:
            self.lastw[w] = ev
            self.reads[w] = []

    def op(self, eng, fn, reads=(), writes=(), inc=True):
        waits = self._deps(eng, reads, writes)
        if inc:
            self.cnt[eng] += 1
            ev = (eng, self.cnt[eng])
        else:
            ev = (eng, self.cnt[eng] + 1)
        self.q[eng].append((waits, fn, (eng, 1) if inc else None))
        self._commit(ev, reads, writes)
        return ev

    def dma(self, eng, fn, reads=(), writes=()):
        ring = self.dsem[eng]
        k = ring[self.dnext[eng] % len(ring)]
        self.dnext[eng] += 1
        waits = self._deps(eng, reads, writes)
        if self.waited[eng].get(k, 0) < self.cnt[k]:
            waits.append((k, self.cnt[k]))
            self.waited[eng][k] = self.cnt[k]
        self.cnt[k] += 16
        ev = (k, self.cnt[k])
        self.q[eng].append((waits, fn, (k, 16)))
        self._commit(ev, reads, writes)
        return ev

    def barrier(self):
        for e in ENGS:
            waits = []
            if e in self.dsem:
                for k in self.dsem[e]:
                    if self.waited[e].get(k, 0) < self.cnt[k]:
                        waits.append((k, self.cnt[k]))
                        self.waited[e][k] = self.cnt[k]
            if e in self.cnt and self.cnt[e] > 0 and self.waited[e].get(e, 0) < self.cnt[e]:
                waits.append((e, self.cnt[e]))
                self.waited[e][e] = self.cnt[e]
            self.q[e].append((waits, "BAR_INC", None))
        self.barcnt += len(ENGS)
        for e in ENGS:
            self.q[e].append(([("__bar", self.barcnt)], None, None))
        self.lastw = {}
        self.reads = {}

    def run(self):
        nc = self.nc
        prog = self

        def replay(ename, eng):
            for waits, fn, inc in prog.q[ename]:
                for k, v in waits:
                    if k == "__bar":
                        eng.wait_ge(prog.bar, v)
                    else:
                        eng.wait_ge(prog.sem[k], v)
                if fn is None:
                    continue
                if fn == "BAR_INC":
                    eng.nop().then_inc(prog.bar, 1)
                    continue
                ins = fn(eng)
                if inc is not None:
                    ins.then_inc(prog.sem[inc[0]], inc[1])

        with nc.Block() as block:
            @block.tensor
            def _(e):
                replay("pe", e)

            @block.scalar
            def _(e):
                replay("act", e)

            @block.vector
            def _(e):
                replay("dve", e)

            @block.gpsimd
            def _(e):
                replay("pool", e)

            @block.sync
            def _(e):
                replay("sp", e)


def head_loc(h):
    if h < 4:
        return h, 0, 0
    if h < 8:
        return h - 4, 64, 0
    if h < 12:
        return 4 + (h - 8), 0, 1
    return 4 + (h - 12), 64, 1


def build(NPT=16, NT=4):
    nc = bass.Bass("TRN2", target_bir_lowering=False)
    NTILES = NPT + 2
    C = NT * 128

    def din(name, shape):
        return nc.dram_tensor(name, list(shape), F32, kind="ExternalInput").ap()

    def dout(name, shape):
        return nc.dram_tensor(name, list(shape), F32, kind="ExternalOutput").ap()

    x_all = din("x_all", [NTILES, 128, D])
    cT_d = din("cT", [128, 16, 17])
    w_adaB = din("w_adaB", [64, 128, 16, 128])
    w_adaG = din("w_adaG", [2, D, D])
    b_adaT = din("b_adaT", [128, 64])
    b_adaG = din("b_adaG", [2, D])
    w_inB = din("w_inB", [34, 128, 16, 128])
    w_kv = din("w_kv", [D, 512])
    w_o = din("w_o", [D, D])
    w_upB = din("w_upB", [64, 128, 16, 128])
    w_down = din("w_down", [DFF, D])
    bias_d = din("bias_tab", [128, 16, 256])
    prevmask_d = din("prevmask", [128, 128])
    sbias_d = din("sbias", [128, 136])
    sinkbc_d = din("sink_bc", [128, 16])
    sinks_d = din("sink_s", [128, 1])
    convw_d = din("conv_wT", [128, 8, 3])
    state_d = din("stateT", [128, 8, 16, 2])
    ident_d = din("ident", [128, 128])
    sel_d = din("sel", [128, 2, 64])
    lnp_d = din("lnp", [4, D])
    flag_d = din("flag", [128, 1])
    cache_k = din("cache_k", [16, 128, 256])
    cache_v = din("cache_v", [16, 128, 256])

    y_all = dout("y_all", [NPT + 1, 128, D])
    kv_last = dout("kv_last", [128, 512])
    conv_last = dout("conv_last", [2, 1024])
    ks_out = dout("ks_out", [16, 128, 256])
    vs_out = dout("vs_out", [16, 128, 256])
    convs_out = dout("convs_out", [32, 1024])
    gscr = nc.dram_tensor("gscr", [17, 2, D], F32, kind="Internal").ap()

    st = ExitStack()
    with st:
        cur = [(nc._sbuf_addr_for_side("left") + 63) // 64 * 64]
        lim = nc._sbuf_addr_for_side("right")

        def sb(name, shape, dt, at=None):
            nbytes = int(np.prod(shape[1:])) * (4 if dt == F32 else 2)
            nbytes = (nbytes + 31) // 32 * 32
            if at is None:
                off = cur[0]
                cur[0] += nbytes
                assert cur[0] <= lim, ("SBUF overflow", name, cur[0], lim)
            else:
                off = at
            return nc.alloc_sbuf_tensor_at(name, list(shape), dt, offset=off), off, nbytes

        def sbp(name, shape, dt):
            return sb(name, shape, dt)[0]

        identb = sbp("identb", [128, 128], BF16)
        identf = sbp("identf", [128, 128], F32)
        modT = sbp("modT", [128, 64, 17], F32)
        bias_t = sbp("bias_t", [128, 16, 256], F32)
        sbias = sbp("sbias", [128, 136], F32)
        sink_bc = sbp("sink_bcs", [128, 16], F32)
        sink_s = sbp("sink_ss", [128, 1], F32)
        prevmask = sbp("prevmasks", [128, 128], F32)
        convw = sbp("convw", [128, 8, 3], F32)
        flag = sbp("flags", [128, 1], F32)
        epsb = sbp("epsb", [128, 1], F32)
        epsb2 = sbp("epsb2", [128, 1], F32)
        carry = sbp("carry", [128, 8, 2], F32)
        kT = sbp("kT", [128, 2, 128 + C], BF16)
        Vb = sbp("Vb", [128, NT + 1, 256], BF16)
        NSL = NT + 1
        stt = sbp("stt", [128, NSL, 4, 6], F32)
        mv = sbp("mv", [128, NSL, 2], F32)
        rstd = sbp("rstd", [128, NSL], F32)
        nb = sbp("nb", [128, NSL], F32)
        xnb2 = sbp("xnb2", [128, D], BF16)
        mx = sbp("mx", [128, 16], F32)
        negm = sbp("negm", [128, 16], F32)
        sm = sbp("sm", [128, 16], F32)
        rc = sbp("rc", [128, 16], F32)
        bc = sbp("bc", [128, 4, D], F32)
        xs = sbp("xs", [128, D], F32)
        xnb = sbp("xnb", [128, D], BF16)
        wB = sbp("wB", [128, 4, 16, 128], BF16)
        wA = sbp("wA", [128, 4, 4, 512], BF16)
        hT = sbp("hT", [128, 16, C], BF16)
        mixT = sbp("mixT", [128, 16, C], BF16)
        rT = sbp("rT", [128, C], BF16)
        tmpq, TQ, TQn = sb("tmpq", [128, 512], F32)
        acc, RA, RAn = sb("acc", [128, NT, D], F32)
        uT, RU, RUn = sb("uT", [128, 32, C], BF16)
        print("SBUF used", cur[0], "of", lim)

        o = [RA]

        ranges = {}

        def ra(name, shape, dt):
            t, off, n = sb(name, shape, dt, at=o[0])
            ranges[name] = (off, n)
            o[0] += n
            assert o[0] <= RA + RAn, ("RA overflow", name)
            return t

        gcs = ra("gcs", [128, C], F32)
        ubuf = ra("ubuf", [128, C + 2], F32)
        tcv = ra("tcv", [128, C], F32)
        us = ra("us", [128, 16, 10], F32)
        ts = ra("ts", [128, 16, 8], F32)
        kvst = ra("kvst", [128, 512], F32)
        ulast_s = ra("ulast_s", [128, 8, 16, 2], F32)
        ulast_p = ra("ulast_p", [128, 8, 2], F32)
        cvst = ra("cvst", [32, 1024], F32)
        o[0] = RA
        ck = ra("ck", [128, 8, 256], BF16)
        cv = ra("cv", [128, 8, 256], BF16)
        KT = ra("KT", [128, 8, 2, 136], BF16)
        Ss = ra("Ss", [128, 8, 138], F32)
        Pf = ra("Pf", [128, 8, 138], F32)
        Pb = ra("Pb", [128, 8, 136], BF16)
        PTc = ra("PTc", [128, 8, 128], BF16)
        PTn = ra("PTn", [8, 8, 128], BF16)
        vnew = ra("vnew", [8, 8, 256], BF16)
        osb = ra("osb", [128, 8, 64], BF16)
        o[0] = RA
        cTs = ra("cTs", [128, 16, 17], F32)
        siluT = ra("siluT", [128, 16, 17], BF16)
        badaT = ra("badaT", [128, 64], F32)
        bG = ra("bG", [17, 2, D], F32)

        o[0] = RU

        def ru(name, shape, dt):
            t, off, n = sb(name, shape, dt, at=o[0])
            ranges[name] = (off, n)
            o[0] += n
            assert o[0] <= RU + RUn, ("RU overflow", name)
            return t

        PT2 = sb("PT2", [128, 4, 2, 128], BF16, at=TQ)[0]
        ranges["PT2"] = (TQ, 2048)
        ranges["tmpq"] = (TQ, 2048)
        qT = ru("qT", [128, 8, C], BF16)
        S = ru("S", [128, 8, 258], F32)
        Pe = ru("Pe", [128, 16, 258], BF16)
        PT = ru("PT", [128, 4, 2, 128], BF16)
        Abuf = ru("Abuf", [128, 1024], BF16)
        qs = ru("qs", [128, 2, 16, 32], BF16)
        selb = ru("selb", [128, 2, 64], BF16)
        o[0] = RU
        gs = ru("gs", [17, 2, D], F32)

        psb = [st.enter_context(nc.psum_tensor("psb%d" % i, [128, 512], F32)) for i in range(8)]
        pscur = [0]
        held = set()

        def nextbank():
            while True:
                b = pscur[0] % 8
                pscur[0] += 1
                if b not in held:
                    return b

        P = Prog(nc, st)
        for name, (off, n) in ranges.items():
            if name == "qT":
                for m in range(8):
                    P.alias("qT%d" % m, off + m * C * 2, C * 2)
            elif name == "gs":
                for v in range(2):
                    P.alias("gs%d" % v, off, n)
            elif name == "S":
                for par in range(2):
                    P.alias("S%d" % par, off + par * (n // 2), n // 2)
            elif name == "Pe":
                for par in range(4):
                    P.alias("Pe%d" % par, off + par * (n // 4), n // 4)
            else:
                P.alias(name, off, n)
        for j in range(NT):
            P.alias("acc%d" % j, RA + j * D * 4, D * 4)
        for m in range(32):
            P.alias("uT%d" % m, RU + m * C * 2, C * 2)

        P.dma("sp", lambda e: e.dma_start(out=identf[:], in_=ident_d), writes=["identf"])
        P.dma("pool", lambda e: e.dma_start(out=identb[:], in_=ident_d), writes=["identb"])
        P.dma("sp", lambda e: e.dma_start(out=bias_t[:], in_=bias_d), writes=["bias_t"])
        P.dma("sp", lambda e: e.dma_start(out=sbias[:], in_=sbias_d), writes=["sbias"])
        P.dma("sp", lambda e: e.dma_start(out=sink_bc[:], in_=sinkbc_d), writes=["sink_bc"])
        P.dma("sp", lambda e: e.dma_start(out=sink_s[:], in_=sinks_d), writes=["sink_s"])
        P.dma("sp", lambda e: e.dma_start(out=prevmask[:], in_=prevmask_d), writes=["prevmask"])
        P.dma("sp", lambda e: e.dma_start(out=convw[:], in_=convw_d), writes=["convw"])
        P.dma("sp", lambda e: e.dma_start(out=flag[:], in_=flag_d), writes=["flag"])
        P.op("dve", lambda e: e.memset(epsb[:], EPS), writes=["epsb"])
        P.op("dve", lambda e: e.memset(epsb2[:], EPS / (ALPHA * ALPHA)), writes=["epsb2"])
        P.op("dve", lambda e: e.memset(carry[:], 0.0), writes=["carry"])

        def stageB(wd, chunk_ids, rhs_fn, rhs_res, ncols_fn, epilogue, post_hook=None):
            n = len(chunk_ids)
            PF = 3

            def issue(i):
                s = i % 4
                P.dma("pool", lambda e, i=i, s=s: e.dma_start(out=wB[:, s], in_=wd[chunk_ids[i]]),
                      writes=["wB%d" % s])

            for i in range(min(PF, n)):
                issue(i)
            for i in range(n):
                if i + PF < n:
                    issue(i + PF)
                s = i % 4
                b = nextbank()
                for kc in range(16):
                    P.op("pe", lambda e, b=b, s=s, kc=kc: e.matmul(
                        ncols_fn(psb[b]), lhsT=wB[:, s, kc, :], rhs=rhs_fn(kc),
                        start=(kc == 0), stop=(kc == 15)),
                        reads=["wB%d" % s] + rhs_res, writes=["ps%d" % b], inc=(kc == 15))
                epilogue(i, chunk_ids[i], b)
                if post_hook is not None:
                    post_hook(i)

        def stageA(wd, row0, nk, tiles, lhs_fn, lhs_res_fn, epilogue, out_fn=lambda ps: ps[:, :], cq_hook=None, ncq=4):
            ng = nk // 4
            for cq in range(ncq):
                if cq_hook is not None:
                    cq_hook(cq)
                banks = [nextbank() for _ in tiles]

                def issue(g, cq=cq):
                    s = g % 4
                    src = wd[row0 + g * 512: row0 + (g + 1) * 512, cq * 512:(cq + 1) * 512]
                    P.dma("pool", lambda e, s=s, src=src: e.dma_start(
                        out=wA[:, s], in_=src.rearrange("(g p) n -> p g n", p=128)),
                        writes=["wA%d" % s])

                for g in range(min(3, ng)):
                    issue(g)
                for g in range(ng):
                    if g + 3 < ng:
                        issue(g + 3)
                    s = g % 4
                    for k4 in range(4):
                        kc = g * 4 + k4
                        for ti, j in enumerate(tiles):
                            P.op("pe", lambda e, b=banks[ti], s=s, k4=k4, kc=kc, j=j: e.matmul(
                                out_fn(psb[b]), lhsT=lhs_fn(kc, j), rhs=wA[:, s, k4, :],
                                start=(kc == 0), stop=(kc == nk - 1)),
                                reads=["wA%d" % s] + lhs_res_fn(kc, j), writes=["ps%d" % banks[ti]],
                                inc=(kc == nk - 1) or (k4 == 3 and ti == len(tiles) - 1))
                for ti, j in enumerate(tiles):
                    epilogue(j, cq, banks[ti])

        def ln_stats(src_ap, src_res, sl=0):
            t = "_%d" % sl
            for i in range(4):
                P.op("dve", lambda e, i=i: e.bn_stats(out=stt[:, sl, i, :], in_=src_ap[:, i * 512:(i + 1) * 512]),
                     reads=src_res, writes=["stt%d" % i + t])
            P.op("dve", lambda e: e.bn_aggr(out=mv[:, sl, :], in_=stt[:, sl].rearrange("p a b -> p (a b)")),
                 reads=["stt%d" % i + t for i in range(4)], writes=["mv" + t])
            P.op("act", lambda e: e.activation(out=rstd[:, sl:sl + 1], in_=mv[:, sl, 1:2], func=ACTF.Sqrt, bias=epsb[:, 0:1]),
                 reads=["mv" + t, "epsb"], writes=["rstd" + t])
            P.op("dve", lambda e: e.reciprocal(out=rstd[:, sl:sl + 1], in_=rstd[:, sl:sl + 1]),
                 reads=["rstd" + t], writes=["rstd" + t])
            P.op("dve", lambda e: e.tensor_scalar(out=nb[:, sl:sl + 1], in0=mv[:, sl, 0:1], scalar1=rstd[:, sl:sl + 1],
                                                  scalar2=-1.0, op0=ALU.mult, op1=ALU.mult),
                 reads=["mv" + t, "rstd" + t], writes=["nb" + t])

        def ln_stats_multi(tl, src_fn, res_fn, eps_t=None, eps_r="epsb"):
            et = epsb if eps_t is None else eps_t
            for j in tl:
                t = "_%d" % j
                src = src_fn(j)
                for i in range(4):
                    P.op("dve", lambda e, i=i, j=j, src=src: e.bn_stats(out=stt[:, j, i, :], in_=src[:, i * 512:(i + 1) * 512]),
                         reads=res_fn(j), writes=["stt%d" % i + t])
                P.op("dve", lambda e, j=j: e.bn_aggr(out=mv[:, j, :], in_=stt[:, j].rearrange("p a b -> p (a b)")),
                     reads=["stt%d" % i + t for i in range(4)], writes=["mv" + t])
            lo, hi = tl[0], tl[-1] + 1
            mvr = ["mv_%d" % j for j in tl]
            rr = ["rstd_%d" % j for j in tl]
            nr = ["nb_%d" % j for j in tl]
            P.op("act", lambda e: e.activation(out=rstd[:, lo:hi], in_=mv[:, lo:hi, 1], func=ACTF.Sqrt, bias=et[:, 0:1]),
                 reads=mvr + [eps_r], writes=rr)
            P.op("dve", lambda e: e.reciprocal(out=rstd[:, lo:hi], in_=rstd[:, lo:hi]), reads=rr, writes=rr)
            P.op("dve", lambda e: e.scalar_tensor_tensor(out=nb[:, lo:hi], in0=mv[:, lo:hi, 0], scalar=-1.0, in1=rstd[:, lo:hi],
                                                         op0=ALU.mult, op1=ALU.mult),
                 reads=mvr + rr, writes=nr)

        xnbs = [xnb, xnb2]

        def norm_A(src_ap, src_res, sl=0, xb=0, stats=True):
            t = "_%d" % sl
            if stats:
                ln_stats(src_ap, src_res, sl)
            P.op("act", lambda e: e.activation(out=xnbs[xb][:], in_=src_ap, func=ACTF.Identity, bias=nb[:, sl:sl + 1],
                                               scale=rstd[:, sl:sl + 1]),
                 reads=src_res + ["rstd" + t, "nb" + t], writes=["xnb%d" % xb])

        def norm_B(dstT, dst_res, col0, is_sample, m_sc, m_sh, xb=0):
            xnb = xnbs[xb]
            modr = "modT_a" if m_sc < 32 else "modT_b"
            for hf in range(2):
                b = nextbank()
                pv = psb[b][:].bitcast(BF16).rearrange("p (k t) -> p k t", k=8)
                for k in range(8):
                    kc = hf * 8 + k
                    P.op("pe", lambda e, k=k, kc=kc, pv=pv: e.transpose(
                        out=pv[:, k, :], in_=xnb[:, kc * 128:(kc + 1) * 128], identity=identb[:]),
                        reads=["xnb%d" % xb, "identb"], writes=["ps%d" % b], inc=(k == 7))
                dst = dstT[:, hf * 8:(hf + 1) * 8, col0:col0 + 128]
                if is_sample:
                    sc_ap = modT[:, m_sc + hf * 8:m_sc + hf * 8 + 8, 1:17].unsqueeze(3).to_broadcast([128, 8, 16, 8])
                    sh_ap = modT[:, m_sh + hf * 8:m_sh + hf * 8 + 8, 1:17].unsqueeze(3).to_broadcast([128, 8, 16, 8])
                    dv = dst.rearrange("p k (n s) -> p k n s", s=8)
                    pvv = pv.rearrange("p k (n s) -> p k n s", s=8)
                else:
                    for k in range(8):
                        kc = hf * 8 + k
                        o_ap = dstT[:, kc, col0:col0 + 128]
                        i_ap = pv[:, k, :]
                        sc1 = modT[:, m_sc + kc, 0:1]
                        sh1 = modT[:, m_sh + kc, 0:1]
                        if k % 4 == 0:
                            P.op("act", lambda e, o_ap=o_ap, i_ap=i_ap, sc1=sc1, sh1=sh1: e.activation(
                                out=o_ap, in_=i_ap, func=ACTF.Identity, bias=sh1, scale=sc1),
                                reads=["ps%d" % b, modr], writes=[dst_res])
                        else:
                            P.op("dve", lambda e, o_ap=o_ap, i_ap=i_ap, sc1=sc1, sh1=sh1: e.tensor_scalar(
                                out=o_ap, in0=i_ap, scalar1=sc1, scalar2=sh1, op0=ALU.mult, op1=ALU.add),
                                reads=["ps%d" % b, modr], writes=[dst_res])
                    continue
                P.op("dve", lambda e, dv=dv, pvv=pvv, sc_ap=sc_ap: e.tensor_tensor(out=dv, in0=pvv, in1=sc_ap, op=ALU.mult),
                     reads=["ps%d" % b, modr], writes=[dst_res])
                P.op("dve", lambda e, dv=dv, sh_ap=sh_ap: e.tensor_tensor(out=dv, in0=dv, in1=sh_ap, op=ALU.add),
                     reads=[dst_res, modr], writes=[dst_res])

        P.dma("sp", lambda e: e.dma_start(out=cTs[:], in_=cT_d), writes=["cTs"])
        P.dma("sp", lambda e: e.dma_start(out=badaT[:], in_=b_adaT), writes=["badaT"])
        for v in range(2):
            P.dma("sp", lambda e, v=v: e.dma_start(out=bG[:, v, :], in_=b_adaG[v:v + 1, :].partition_broadcast(17)),
                  writes=["bG"])
        P.op("act", lambda e: e.activation(out=siluT[:], in_=cTs[:], func=ACTF.Silu), reads=["cTs"], writes=["siluT"])

        def ada_epi(i, m, b):
            P.op("act", lambda e, m=m, b=b: e.activation(out=modT[:, m, :], in_=psb[b][:, 0:17], func=ACTF.Identity,
                                                          bias=badaT[:, m:m + 1]),
                 reads=["ps%d" % b, "badaT"], writes=["modT_a" if m < 32 else "modT_b"])

        tiles_all = [("h", 0)] + [("p", i) for i in range(NPT)] + [("s", 0)]
        if NTILES % NT == 2 and NTILES >= 2 * NT + 2:
            nfull = NTILES // NT - 1
            sizes = [NT] * nfull + [NT - 1, 3]
        else:
            sizes = [NT] * (NTILES // NT) + ([NTILES % NT] if NTILES % NT else [])
        blocks, _t0 = [], 0
        for _n in sizes:
            blocks.append(list(range(_t0, _t0 + _n)))
            _t0 += _n
        assert _t0 == NTILES
        ln0_done = set()

        def ln0_A(g):
            P.dma("sp", lambda e, g=g: e.dma_start(out=xs[:], in_=x_all[g]), writes=["xs"])
            norm_A(xs[:], ["xs"], sl=NT)

        def ln0_B(g, j):
            norm_B(hT, "hT%d" % j, j * 128, tiles_all[g][0] == "s", 16, 0)

        def ada_hook(i):
            b0 = blocks[0]
            if i % 8 == 3:
                j = i // 8
                if j < len(b0):
                    ln0_B(b0[j], j)
                if j + 1 < len(b0):
                    ln0_A(b0[j + 1])

        for part, m0 in ((0, 16), (1, 48)):
            stageB(w_adaB, list(range(part * 32, part * 32 + 32)), lambda kc: siluT[:, kc, :], ["siluT"],
                   lambda ps: ps[:, 0:17], ada_epi, post_hook=(ada_hook if part == 1 else None))
            mr = "modT_a" if part == 0 else "modT_b"
            P.op("dve", lambda e, m0=m0: e.tensor_scalar(out=modT[:, m0:m0 + 16, :], in0=modT[:, m0:m0 + 16, :],
                                                         scalar1=1.0, scalar2=None, op0=ALU.add),
                 reads=[mr], writes=[mr])
            if part == 0:
                ln0_A(blocks[0][0])
                ln0_done.add(0)
        for v in range(2):
            def g_epi(j, cq, b, v=v):
                P.op("dve", lambda e: e.tensor_tensor(out=gs[:, v, cq * 512:(cq + 1) * 512], in0=psb[b][0:17, :],
                                                      in1=bG[:, v, cq * 512:(cq + 1) * 512], op=ALU.add),
                     reads=["ps%d" % b, "bG"], writes=["gs%d" % v])
            stageA(w_adaG[v], 0, 16, [0], lambda kc, j: siluT[:, kc, :], lambda kc, j: ["siluT"], g_epi,
                   out_fn=lambda ps: ps[0:17, :])
            if v == 1:
                P.op("dve", lambda e: e.tensor_scalar(out=gs[:, 1, :], in0=gs[:, 1, :], scalar1=1.0 / ALPHA, scalar2=None,
                                                      op0=ALU.mult), reads=["gs1"], writes=["gs1"])
            P.dma("sp", lambda e, v=v: e.dma_start(out=gscr[:, v, :], in_=gs[:, v, :]), reads=["gs%d" % v],
                  writes=["gscr"])

        QSCALE = 0.125

        def load_bc(slot, src_row_ap, res):
            P.dma("sp", lambda e: e.dma_start(out=bc[:, slot, :], in_=src_row_ap.partition_broadcast(128)),
                  reads=["gscr"], writes=[res])

        def load_bc_sample(slot, v, res):
            for n in range(16):
                P.dma("sp", lambda e, n=n: e.dma_start(
                    out=bc[n * 8:(n + 1) * 8, slot, :], in_=gscr[1 + n:2 + n, v, :].partition_broadcast(8)),
                    reads=["gscr"], writes=[res])

        def do_block(bi, blk):
            nb_t = len(blk)
            kinds = [tiles_all[g][0] for g in blk]
            main = [j for j in range(nb_t) if kinds[j] != "h"]
            prm = [j for j in range(nb_t) if kinds[j] == "p"]
            seq = [j for j in range(nb_t) if kinds[j] in ("h", "p")]
            smp = [j for j in range(nb_t) if kinds[j] == "s"]
            Cb = nb_t * 128
            Cp = len(seq) * 128
            c_main0 = main[0] * 128
            has_sample = len(smp) > 0
            last_prompt_j = None
            for j in prm:
                if tiles_all[blk[j]][1] == NPT - 1:
                    last_prompt_j = j

            if bi not in ln0_done:
                for j in range(nb_t):
                    ln0_A(blk[j])
                    ln0_B(blk[j], j)
            hres = ["hT%d" % j for j in range(nb_t)]

            def kv_epi(j, cq, b):
                P.op("act", lambda e: e.activation(out=Vb[:, j + 1, :], in_=psb[b][:, 256:512], func=ACTF.Copy),
                     reads=["ps%d" % b], writes=["V%d" % (j + 1)])
                if j == last_prompt_j or kinds[j] == "s":
                    P.op("act", lambda e: e.activation(out=kvst[:], in_=psb[b][:, :], func=ACTF.Copy),
                         reads=["ps%d" % b], writes=["kvst"])
                    if kinds[j] == "p":
                        P.dma("sp", lambda e: e.dma_start(out=kv_last, in_=kvst[:]), reads=["kvst"], writes=["kv_last"])
                    else:
                        for n in range(16):
                            P.dma("sp", lambda e, n=n: e.dma_start(out=ks_out[n, 120:128, :],
                                                                    in_=kvst[n * 8:(n + 1) * 8, 0:256]),
                                  reads=["kvst"], writes=["ks_out"])
                            P.dma("sp", lambda e, n=n: e.dma_start(out=vs_out[n, 120:128, :],
                                                                    in_=kvst[n * 8:(n + 1) * 8, 256:512]),
                                  reads=["kvst"], writes=["vs_out"])

            stageA(w_kv, 0, 16, list(range(nb_t)), lambda kc, j: hT[:, kc, j * 128:(j + 1) * 128],
                   lambda kc, j: ["hT%d" % j], kv_epi, ncq=1)
            if has_sample:
                P.dma("sp", lambda e: e.dma_start(out=ks_out[:, 0:120, :], in_=cache_k[:, 8:128, :]), writes=["ks_out2"])
                P.dma("sp", lambda e: e.dma_start(out=vs_out[:, 0:120, :], in_=cache_v[:, 8:128, :]), writes=["vs_out2"])

            def win_epi(i, m, b):
                if m < 8:
                    P.op("act", lambda e: e.activation(out=qT[:, m, 0:Cb], in_=psb[b][:, 0:Cb], func=ACTF.Copy,
                                                       scale=QSCALE),
                         reads=["ps%d" % b], writes=["qT%d" % m])
                elif m < 10:
                    P.op("act", lambda e: e.activation(out=kT[:, m - 8, 128:128 + Cb], in_=psb[b][:, 0:Cb],
                                                       func=ACTF.Copy),
                         reads=["ps%d" % b], writes=["kT%d" % (m - 8)])
                else:
                    c, r = divmod(m - 10, 3)
                    if r == 0:
                        P.op("act", lambda e: e.activation(out=gcs[:, 0:Cb], in_=psb[b][:, 0:Cb], func=ACTF.Copy),
                             reads=["ps%d" % b], writes=["gcs"])
                    elif r == 1:
                        if Cp > 0:
                            P.op("dve", lambda e: e.tensor_copy(out=ubuf[:, 0:2], in_=carry[:, c, :]),
                                 reads=["carry"], writes=["ubuf"])
                            P.op("dve", lambda e: e.tensor_tensor(out=ubuf[:, 2:2 + Cp], in0=psb[b][:, 0:Cp],
                                                                  in1=gcs[:, 0:Cp], op=ALU.mult),
                                 reads=["ps%d" % b, "gcs", "ubuf"], writes=["ubuf"])
                            if kinds[0] == "h":
                                P.op("dve", lambda e: e.tensor_scalar(out=ubuf[:, 128:130], in0=ubuf[:, 128:130],
                                                                      scalar1=flag[:, 0:1], scalar2=None, op0=ALU.mult),
                                     reads=["ubuf", "flag"], writes=["ubuf"])
                            P.op("dve", lambda e: e.tensor_copy(out=carry[:, c, :], in_=ubuf[:, Cp:Cp + 2]),
                                 reads=["ubuf"], writes=["carry"])
                            if last_prompt_j is not None:
                                P.op("dve", lambda e: e.tensor_copy(out=ulast_p[:, c, :], in_=ubuf[:, Cp:Cp + 2]),
                                     reads=["ubuf"], writes=["ulast_p"])
                        if has_sample:
                            P.dma("sp", lambda e: e.dma_start(out=us[:, :, 0:2], in_=state_d[:, c]), writes=["us"])
                            P.op("dve", lambda e: e.tensor_tensor(
                                out=us[:, :, 2:10], in0=psb[b][:, Cp:Cb].rearrange("p (n s) -> p n s", s=8),
                                in1=gcs[:, Cp:Cb].rearrange("p (n s) -> p n s", s=8), op=ALU.mult),
                                reads=["ps%d" % b, "gcs", "us"], writes=["us"])
                            P.op("dve", lambda e: e.tensor_copy(out=ulast_s[:, c], in_=us[:, :, 8:10]),
                                 reads=["us"], writes=["ulast_s"])
                    else:
                        if Cp > 0:
                            P.op("dve", lambda e: e.tensor_scalar(out=tcv[:, 0:Cp], in0=ubuf[:, 0:Cp],
                                                                   scalar1=convw[:, c, 0:1], scalar2=None, op0=ALU.mult),
                                 reads=["ubuf", "convw"], writes=["tcv"])
                            for tap in (1, 2):
                                P.op("dve", lambda e, tap=tap: e.scalar_tensor_tensor(
                                    out=tcv[:, 0:Cp], in0=ubuf[:, tap:tap + Cp], scalar=convw[:, c, tap:tap + 1],
                                    in1=tcv[:, 0:Cp], op0=ALU.mult, op1=ALU.add),
                                    reads=["ubuf", "convw", "tcv"], writes=["tcv"])
                            P.op("dve", lambda e: e.tensor_tensor(out=mixT[:, 8 + c, 0:Cp], in0=psb[b][:, 0:Cp],
                                                                  in1=tcv[:, 0:Cp], op=ALU.mult),
                                 reads=["ps%d" % b, "tcv"], writes=["mix%d" % (8 + c)])
                        if has_sample:
                            P.op("dve", lambda e: e.tensor_scalar(out=ts[:], in0=us[:, :, 0:8],
                                                                   scalar1=convw[:, c, 0:1], scalar2=None, op0=ALU.mult),
                                 reads=["us", "convw"], writes=["ts"])
                            for tap in (1, 2):
                                P.op("dve", lambda e, tap=tap: e.scalar_tensor_tensor(
                                    out=ts[:], in0=us[:, :, tap:tap + 8], scalar=convw[:, c, tap:tap + 1],
                                    in1=ts[:], op0=ALU.mult, op1=ALU.add),
                                    reads=["us", "convw", "ts"], writes=["ts"])
                            P.op("dve", lambda e: e.tensor_tensor(
                                out=mixT[:, 8 + c, Cp:Cb].rearrange("p (n s) -> p n s", s=8),
                                in0=psb[b][:, Cp:Cb].rearrange("p (n s) -> p n s", s=8), in1=ts[:], op=ALU.mult),
                                reads=["ps%d" % b, "ts"], writes=["mix%d" % (8 + c)])

            def attn_front(ui):
                j, q = units[ui]
                par = ui % 2
                p0 = par * 4
                tri = ui % 4
                e0 = tri * 4
                gp = tiles_all[blk[j]][1]
                Sr, Per = "S%d" % par, "Pe%d" % tri
                P.op("dve", lambda e: e.tensor_copy(out=S[:, p0:p0 + 4, 256:257],
                                                    in_=sink_bc[:, 4 * q:4 * q + 4].unsqueeze(2)),
                     reads=["sink_bc"], writes=[Sr])
                for pr in range(2):
                    b = nextbank()
                    for hh in range(2):
                        h = 4 * q + pr * 2 + hh
                        qc, off, kch = head_loc(h)
                        P.op("pe", lambda e, b=b, hh=hh, qc=qc, off=off, kch=kch: e.matmul(
                            psb[b][:, hh * 256:(hh + 1) * 256],
                            lhsT=qT[off:off + 64, qc, j * 128:(j + 1) * 128],
                            rhs=kT[off:off + 64, kch, j * 128:j * 128 + 256], start=True, stop=True),
                            reads=["qT%d" % qc, "kT%d" % kch, "kTp"], writes=["ps%d" % b], inc=(hh == 1))
                    h0 = 4 * q + pr * 2
                    P.op("dve", lambda e, b=b, pr=pr, h0=h0: e.tensor_tensor(
                        out=S[:, p0 + pr * 2:p0 + pr * 2 + 2, 0:256], in0=psb[b][:].rearrange("p (a k) -> p a k", a=2),
                        in1=bias_t[:, h0:h0 + 2, :], op=ALU.add),
                        reads=["ps%d" % b, "bias_t"], writes=[Sr])
                if gp == 0:
                    P.op("dve", lambda e: e.tensor_tensor(
                        out=S[:, p0:p0 + 4, 0:128], in0=S[:, p0:p0 + 4, 0:128],
                        in1=prevmask[:].unsqueeze(1).to_broadcast([128, 4, 128]), op=ALU.add),
                        reads=[Sr, "prevmask"], writes=[Sr])
                P.op("dve", lambda e: e.reduce_max(out=mx[:, e0:e0 + 4], in_=S[:, p0:p0 + 4, 0:257], axis=AX.X),
                     reads=[Sr], writes=["mx%d" % tri])
                P.op("dve", lambda e: e.tensor_scalar(out=negm[:, e0:e0 + 4], in0=mx[:, e0:e0 + 4], scalar1=-1.0,
                                                      scalar2=None, op0=ALU.mult),
                     reads=["mx%d" % tri], writes=["negm%d" % tri])
                for hl in range(4):
                    P.op("act", lambda e, hl=hl: e.activation(
                        out=Pe[:, e0 + hl, 0:257], in_=S[:, p0 + hl, 0:257], func=ACTF.Exp,
                        bias=negm[:, e0 + hl:e0 + hl + 1], accum_out=sm[:, e0 + hl:e0 + hl + 1]),
                        reads=[Sr, "negm%d" % tri], writes=[Per, "sm%d" % tri])
                P.op("dve", lambda e: e.reciprocal(out=rc[:, e0:e0 + 4], in_=sm[:, e0:e0 + 4]),
                     reads=["sm%d" % tri], writes=["rc%d" % tri])

            PTs = [PT, PT2]

            def attn_back1(ui):
                j, q = units[ui]
                tri = ui % 4
                e0 = tri * 4
                Per = "Pe%d" % tri
                ptb = ui % 2
                PTr = "PT" if ptb == 0 else "PT2"
                b = nextbank()
                pv = psb[b][:].bitcast(BF16).rearrange("p (a c t) -> p a c t", a=4, c=2)
                for a_ in range(4):
                    for c2 in range(2):
                        P.op("pe", lambda e, pv=pv, a_=a_, c2=c2: e.transpose(
                            out=pv[:, a_, c2, :], in_=Pe[:, e0 + a_, c2 * 128:(c2 + 1) * 128], identity=identb[:]),
                            reads=[Per, "identb"], writes=["ps%d" % b], inc=(a_ == 3 and c2 == 1))
                P.op("act", lambda e, pv=pv: e.activation(out=PTs[ptb][:, 0:4], in_=pv, func=ACTF.Copy),
                     reads=["ps%d" % b], writes=[PTr])

            def attn_back2(ui):
                j, q = units[ui]
                tri = ui % 4
                e0 = tri * 4
                ptb = ui % 2
                PTr = "PT" if ptb == 0 else "PT2"
                b2 = nextbank()
                for hl in range(4):
                    for c2 in range(2):
                        P.op("pe", lambda e, b2=b2, hl=hl, c2=c2: e.matmul(
                            psb[b2][:, hl * 64:(hl + 1) * 64], lhsT=PTs[ptb][:, hl, c2, :],
                            rhs=Vb[:, j + c2, q * 64:(q + 1) * 64], start=(c2 == 0), stop=(c2 == 1)),
                            reads=[PTr, "V%d" % (j + c2)], writes=["ps%d" % b2], inc=(hl == 3 and c2 == 1))
                P.op("dve", lambda e, b2=b2: e.tensor_tensor(
                    out=Abuf[:, q * 256:(q + 1) * 256].rearrange("p (a d) -> p a d", d=64),
                    in0=psb[b2][:, 0:256].rearrange("p (a d) -> p a d", d=64),
                    in1=rc[:, e0:e0 + 4].unsqueeze(2).to_broadcast([128, 4, 64]), op=ALU.mult),
                    reads=["ps%d" % b2, "rc%d" % tri], writes=["Abuf"])

            def attn_back3(ui):
                j, q = units[ui]
                if q == 3:
                    b3 = nextbank()
                    pv3 = psb[b3][:].bitcast(BF16).rearrange("p (k t) -> p k t", k=8)
                    for k in range(8):
                        P.op("pe", lambda e, pv3=pv3, k=k: e.transpose(out=pv3[:, k, :], in_=Abuf[:, k * 128:(k + 1) * 128],
                                                                       identity=identb[:]),
                             reads=["Abuf", "identb"], writes=["ps%d" % b3], inc=(k == 7))
                    P.op("act", lambda e, pv3=pv3: e.activation(out=mixT[:, 0:8, j * 128:(j + 1) * 128], in_=pv3,
                                                                func=ACTF.Copy),
                         reads=["ps%d" % b3], writes=["mixA%d" % j])

            units = [(j, q) for j in prm for q in range(4)]
            apieces = []
            nu = len(units)
            for step in range(nu + 4):
                if step < nu:
                    apieces.append(lambda ui=step: attn_front(ui))
                if 0 <= step - 2 < nu:
                    apieces.append(lambda ui=step - 2: attn_back1(ui))
                if 0 <= step - 4 < nu and units[step - 4][1] == 3:
                    apieces.append(lambda ui=step - 4: attn_back3(ui))
                if 0 <= step - 3 < nu:
                    apieces.append(lambda ui=step - 3: attn_back2(ui))
            n_ap = len(apieces)
            emitted = [0]

            def win_hook(i):
                if i < 9:
                    return
                target = (n_ap * (i - 8) + 24) // 25
                while emitted[0] < target and apieces:
                    apieces.pop(0)()
                    emitted[0] += 1

            stageB(w_inB, list(range(34)), lambda kc: hT[:, kc, 0:Cb], hres, lambda ps: ps[:, 0:Cb], win_epi,
                   post_hook=win_hook)

            if last_prompt_j is not None:
                b = nextbank()
                b2 = nextbank()
                for c in range(8):
                    bb = b if c < 4 else b2
                    P.op("pe", lambda e, c=c, bb=bb: e.transpose(out=psb[bb][0:2, (c % 4) * 128:(c % 4 + 1) * 128],
                                                                   in_=ulast_p[:, c, :], identity=identf[:]),
                         reads=["ulast_p", "identf"], writes=["ps%d" % bb], inc=(c % 4 == 3))
                for hf, bb in enumerate((b, b2)):
                    P.op("act", lambda e, hf=hf, bb=bb: e.activation(out=cvst[0:2, hf * 512:(hf + 1) * 512],
                                                                     in_=psb[bb][0:2, :], func=ACTF.Copy),
                         reads=["ps%d" % bb], writes=["cvst"])
                P.dma("sp", lambda e: e.dma_start(out=conv_last, in_=cvst[0:2, :]), reads=["cvst"], writes=["conv_last"])
            if has_sample:
                b = nextbank()
                b2 = nextbank()
                for c in range(8):
                    bb = b if c < 4 else b2
                    P.op("pe", lambda e, c=c, bb=bb: e.transpose(
                        out=psb[bb][0:32, (c % 4) * 128:(c % 4 + 1) * 128],
                        in_=ulast_s[:, c].rearrange("p n r -> p (n r)"), identity=identf[:]),
                        reads=["ulast_s", "identf"], writes=["ps%d" % bb], inc=(c % 4 == 3))
                for hf, bb in enumerate((b, b2)):
                    P.op("act", lambda e, hf=hf, bb=bb: e.activation(out=cvst[0:32, hf * 512:(hf + 1) * 512],
                                                                     in_=psb[bb][0:32, :], func=ACTF.Copy),
                         reads=["ps%d" % bb, "conv_last"], writes=["cvst"])
                P.dma("sp", lambda e: e.dma_start(out=convs_out, in_=cvst[0:32, :]), reads=["cvst"], writes=["convs_out"])

            while apieces:
                apieces.pop(0)()

            if len(seq) > 0 and bi + 1 < len(blocks):
                lj = seq[-1]
                P.op("dve", lambda e, lj=lj: e.tensor_copy(out=kT[:, :, 0:128], in_=kT[:, :, 128 + lj * 128:256 + lj * 128]),
                     reads=["kT0", "kT1", "kTp"], writes=["kTp"])
                P.op("dve", lambda e, lj=lj: e.tensor_copy(out=Vb[:, 0, :], in_=Vb[:, lj + 1, :]),
                     reads=["V%d" % (lj + 1), "V0"], writes=["V0"])

            if has_sample:
                js = smp[0]
                cs0 = js * 128
                P.dma("pool", lambda e: e.dma_start(out=selb[:], in_=sel_d), writes=["selb"])
                for grp in range(2):
                    P.op("dve", lambda e, grp=grp: e.tensor_copy(
                        out=qs[:, grp].rearrange("p n (g s) -> p n g s", s=8),
                        in_=qT[:, grp * 4:grp * 4 + 4, cs0:cs0 + 128].rearrange("p g (n s) -> p n g s", s=8)),
                        reads=["qT%d" % (grp * 4 + i) for i in range(4)], writes=["qs"])
                for g in range(2):
                    n0 = g * 8
                    P.dma("pool", lambda e, n0=n0: e.dma_start(out=ck[:], in_=cache_k[n0:n0 + 8].rearrange("n k d -> k n d")),
                          writes=["ck"])
                    P.dma("pool", lambda e, n0=n0: e.dma_start(out=cv[:], in_=cache_v[n0:n0 + 8].rearrange("n k d -> k n d")),
                          writes=["cv"])
                    for n in range(8):
                        r0 = cs0 - 0 + 0
                        P.dma("sp", lambda e, n=n, n0=n0: e.dma_start(out=vnew[0:8, n, :],
                                                                       in_=Vb[(n0 + n) * 8:(n0 + n + 1) * 8, js + 1, :]),
                              reads=["V%d" % (js + 1)], writes=["vnew"])
                    for q4 in range(2):
                        b = nextbank()
                        pv = psb[b][:].bitcast(BF16).rearrange("p (a c t) -> p a c t", a=4, c=2)
                        for a in range(4):
                            for c2 in range(2):
                                P.op("pe", lambda e, pv=pv, a=a, c2=c2, q4=q4: e.transpose(
                                    out=pv[:, a, c2, :], in_=ck[:, q4 * 4 + a, c2 * 128:(c2 + 1) * 128], identity=identb[:]),
                                    reads=["ck", "identb"], writes=["ps%d" % b], inc=(a == 3 and c2 == 1))
                        P.op("act", lambda e, pv=pv, q4=q4: e.activation(out=KT[:, q4 * 4:q4 * 4 + 4, :, 0:128], in_=pv,
                                                                         func=ACTF.Copy),
                             reads=["ps%d" % b], writes=["KT"])
                    for c2 in range(2):
                        P.op("dve", lambda e, c2=c2, n0=n0: e.tensor_copy(
                            out=KT[:, :, c2, 128:136],
                            in_=kT[:, c2, 128 + cs0 + n0 * 8:128 + cs0 + n0 * 8 + 64].rearrange("p (n s) -> p n s", s=8)),
                            reads=["kT%d" % c2, "KT"], writes=["KT"])
                    sbk = [nextbank() for _ in range(3)]
                    for n in range(8):
                        b = sbk[n // 3]
                        for jj in range(4):
                            off = 64 * (jj % 2)
                            cb0 = 0 if jj < 2 else 4
                            col = cs0 + (n0 + n) * 8
                            P.op("pe", lambda e, b=b, n=n, jj=jj, off=off, cb0=cb0, col=col, n0=n0: e.matmul(
                                psb[b][32 * jj:32 * jj + 32, (n % 3) * 136:(n % 3) * 136 + 136],
                                lhsT=qs[off:off + 64, cb0 // 4, n0 + n, :],
                                rhs=KT[off:off + 64, n, jj // 2, :], start=True, stop=True, tile_position=(off, 32 * jj)),
                                reads=["KT", "qs"], writes=["ps%d" % b],
                                inc=(jj == 3 and (n % 3 == 2 or n == 7)))
                    for gi in range(3):
                        nn = 3 if gi < 2 else 2
                        b = sbk[gi]
                        P.op("dve", lambda e, b=b, gi=gi, nn=nn: e.tensor_tensor(
                            out=Ss[:, gi * 3:gi * 3 + nn, 0:136],
                            in0=psb[b][:, 0:nn * 136].rearrange("p (n k) -> p n k", k=136),
                            in1=sbias[:].unsqueeze(1).to_broadcast([128, nn, 136]), op=ALU.add),
                            reads=["ps%d" % b, "sbias"], writes=["Ss"])
                    P.op("dve", lambda e: e.tensor_copy(out=Ss[:, :, 136:137],
                                                        in_=sink_s[:, 0:1].unsqueeze(1).to_broadcast([128, 8, 1])),
                         reads=["sink_s"], writes=["Ss"])
                    P.op("dve", lambda e: e.reduce_max(out=mx[:, 0:8], in_=Ss[:, :, 0:137], axis=AX.X), reads=["Ss"], writes=["mx0", "mx1"])
                    P.op("dve", lambda e: e.tensor_scalar(out=negm[:, 0:8], in0=mx[:, 0:8], scalar1=-1.0, scalar2=None, op0=ALU.mult),
                         reads=["mx0", "mx1"], writes=["negm0", "negm1"])
                    for n in range(8):
                        P.op("act", lambda e, n=n: e.activation(out=Pf[:, n, 0:137], in_=Ss[:, n, 0:137], func=ACTF.Exp,
                                                                bias=negm[:, n:n + 1], accum_out=sm[:, n:n + 1]),
                             reads=["Ss", "negm0", "negm1"], writes=["Pf", "sm0", "sm1"])
                    P.op("dve", lambda e: e.reciprocal(out=rc[:, 0:8], in_=sm[:, 0:8]), reads=["sm0", "sm1"], writes=["rc0", "rc1"])
                    P.op("dve", lambda e: e.tensor_tensor(out=Pb[:], in0=Pf[:, :, 0:136],
                                                          in1=rc[:, 0:8].unsqueeze(2).to_broadcast([128, 8, 136]), op=ALU.mult),
                         reads=["Pf", "rc0", "rc1"], writes=["Pb"])
                    b1 = nextbank()
                    b2 = nextbank()
                    pv1 = psb[b1][:].bitcast(BF16).rearrange("p (n t) -> p n t", n=8)
                    pv2 = psb[b2][:].bitcast(BF16).rearrange("p (n t) -> p n t", n=8)
                    for n in range(8):
                        P.op("pe", lambda e, n=n, pv1=pv1: e.transpose(out=pv1[:, n, :], in_=Pb[:, n, 0:128], identity=identb[:]),
                             reads=["Pb", "identb"], writes=["ps%d" % b1], inc=(n == 7))
                    for n in range(8):
                        P.op("pe", lambda e, n=n, pv2=pv2: e.transpose(out=pv2[0:8, n, :], in_=Pb[:, n, 128:136], identity=identb[:]),
                             reads=["Pb", "identb"], writes=["ps%d" % b2], inc=(n == 7))
                    P.op("act", lambda e, pv1=pv1: e.activation(out=PTc[:], in_=pv1, func=ACTF.Copy),
                         reads=["ps%d" % b1], writes=["PTc"])
                    P.op("act", lambda e, pv2=pv2: e.activation(out=PTn[0:8], in_=pv2[0:8], func=ACTF.Copy),
                         reads=["ps%d" % b2], writes=["PTn"])
                    bo_ = nextbank()
                    for n in range(8):
                        for jj in range(4):
                            P.op("pe", lambda e, n=n, jj=jj, bo_=bo_: e.matmul(
                                psb[bo_][32 * jj:32 * jj + 32, n * 64:(n + 1) * 64],
                                lhsT=PTc[:, n, 32 * jj:32 * jj + 32], rhs=cv[:, n, jj * 64:(jj + 1) * 64],
                                start=True, stop=False, tile_position=(0, 32 * jj)),
                                reads=["cv", "PTc"], writes=["ps%d" % bo_], inc=False)
                            P.op("pe", lambda e, n=n, jj=jj, bo_=bo_: e.matmul(
                                psb[bo_][32 * jj:32 * jj + 32, n * 64:(n + 1) * 64],
                                lhsT=PTn[0:8, n, 32 * jj:32 * jj + 32], rhs=vnew[0:8, n, jj * 64:(jj + 1) * 64],
                                start=False, stop=True, tile_position=(0, 32 * jj)),
                                reads=["vnew", "PTn"], writes=["ps%d" % bo_], inc=(n == 7 and jj == 3))
                    P.op("act", lambda e, bo_=bo_: e.activation(out=osb[:], in_=psb[bo_][:].rearrange("p (n d) -> p n d", d=64),
                                                       func=ACTF.Copy),
                         reads=["ps%d" % bo_], writes=["osb"])
                    by_ = nextbank()
                    for n in range(8):
                        for par in range(2):
                            P.op("pe", lambda e, n=n, par=par, by_=by_: e.matmul(
                                psb[by_][64 * par:64 * par + 64, n * 64:(n + 1) * 64],
                                lhsT=osb[:, n, :], rhs=selb[:, par, :], start=True, stop=True,
                                tile_position=(0, 64 * par)),
                                reads=["osb", "selb"], writes=["ps%d" % by_], inc=(n == 7 and par == 1))
                    P.op("act", lambda e, n0=n0, by_=by_: e.activation(
                        out=mixT[:, 0:8, cs0 + n0 * 8:cs0 + n0 * 8 + 64].rearrange("p c (n s) -> p n c s", s=8),
                        in_=psb[by_][:].rearrange("p (n c s) -> p n c s", c=8, s=8), func=ACTF.Copy),
                        reads=["ps%d" % by_], writes=["mixA%d" % js])

            load_bc(0, gscr[0:1, 0, :], "bc0")
            if has_sample:
                load_bc_sample(1, 0, "bc1")
            load_bc(2, lnp_d[0:1, :], "bc2")
            load_bc(3, lnp_d[1:2, :], "bc3")

            def wo_epi(j, cq, b):
                slot = 1 if kinds[j] == "s" else 0
                P.op("dve", lambda e: e.tensor_tensor(out=acc[:, j, cq * 512:(cq + 1) * 512], in0=psb[b][:, :],
                                                      in1=bc[:, slot, cq * 512:(cq + 1) * 512], op=ALU.mult),
                     reads=["ps%d" % b, "bc%d" % slot], writes=["acc%d" % j])

            if has_sample:
                xsl, xres = [xs[:], xs[:]], ["xs", "xs"]
            else:
                xsl, xres = [xs[:], bc[:, 1, :]], ["xs", "bc1"]
            groups = [main]
            xcount = [0]
            xslot = {}

            def xload(j):
                k = xcount[0] % 2
                xcount[0] += 1
                xslot[j] = k
                P.dma("sp", lambda e, g=blk[j], xa=xsl[k]: e.dma_start(out=xa, in_=x_all[g]), writes=[xres[k]])

            def chain(tl):
                for j in tl:
                    ar = ["acc%d" % j]
                    if j not in xslot:
                        xload(j)
                    xa, xr = xsl[xslot[j]], xres[xslot[j]]
                    P.op("dve", lambda e, j=j, xa=xa: e.scalar_tensor_tensor(out=acc[:, j, :], in0=xa, scalar=ALPHA,
                                                                             in1=acc[:, j, :], op0=ALU.mult, op1=ALU.add),
                         reads=[xr] + ar, writes=ar)
                ln_stats_multi(tl, lambda j: acc[:, j, :], lambda j: ["acc%d" % j])
                for j in tl:
                    ar = ["acc%d" % j]
                    P.op("act", lambda e, j=j: e.activation(out=acc[:, j, :], in_=acc[:, j, :], func=ACTF.Identity,
                                                            bias=nb[:, j:j + 1], scale=rstd[:, j:j + 1]),
                         reads=ar + ["rstd_%d" % j, "nb_%d" % j], writes=ar)
                for j in tl:
                    ar = ["acc%d" % j]
                    P.op("dve", lambda e, j=j: e.tensor_tensor(out=acc[:, j, :], in0=acc[:, j, :], in1=bc[:, 2, :], op=ALU.mult),
                         reads=ar + ["bc2"], writes=ar)
                    P.op("dve", lambda e, j=j: e.tensor_tensor(out=acc[:, j, :], in0=acc[:, j, :], in1=bc[:, 3, :], op=ALU.add),
                         reads=ar + ["bc3"], writes=ar)
                ln_stats_multi(tl, lambda j: acc[:, j, :], lambda j: ["acc%d" % j])
                def cA(ti):
                    j = tl[ti]
                    norm_A(acc[:, j, :], ["acc%d" % j], sl=j, xb=ti % 2, stats=False)

                def cB(ti):
                    j = tl[ti]
                    norm_B(hT, "hT%d" % j, j * 128, kinds[j] == "s", 48, 32, xb=ti % 2)

                for ti in range(len(tl)):
                    cA(ti)
                    if ti >= 1:
                        cB(ti - 1)
                cB(len(tl) - 1)

            if not has_sample:
                for j in groups[0][:2]:
                    xload(j)
            for gi, grp in enumerate(groups):
                stageA(w_o, 0, 16, grp, lambda kc, j: mixT[:, kc, j * 128:(j + 1) * 128],
                       lambda kc, j: (["mixA%d" % j] if kc < 8 else ["mix%d" % kc]), wo_epi)
                if has_sample:
                    for j in grp:
                        xload(j)
                        chain([j])
                else:
                    chain(grp)
                    if gi + 1 < len(groups):
                        for j in groups[gi + 1]:
                            xload(j)

            load_bc(0, gscr[0:1, 1, :], "bc0")
            if has_sample:
                load_bc_sample(1, 1, "bc1")
            load_bc(2, lnp_d[2:3, :], "bc2")
            load_bc(3, lnp_d[3:4, :], "bc3")
            Cm = Cb - c_main0
            h2res = ["hT%d" % j for j in main]
            for fh in range(2):
                def up_epi(i, m, b):
                    ml = m - fh * 32
                    P.op("act", lambda e: e.activation(out=rT[:, 0:Cm], in_=psb[b][:, 0:Cm], func=ACTF.Square),
                         reads=["ps%d" % b], writes=["rT"])
                    P.op("dve", lambda e: e.scalar_tensor_tensor(out=uT[:, ml, c_main0:Cb], in0=psb[b][:, 0:Cm], scalar=0.0,
                                                                  in1=rT[:, 0:Cm], op0=ALU.is_gt, op1=ALU.mult),
                         reads=["ps%d" % b, "rT"], writes=["uT%d" % ml])

                uphook = None
                if fh == 1 and bi + 1 < len(blocks):
                    def uphook(i, nblk0=blocks[bi + 1]):
                        if i == 24:
                            ln0_A(nblk0[0])
                stageB(w_upB, list(range(fh * 32, fh * 32 + 32)), lambda kc: hT[:, kc, c_main0:Cb], h2res,
                       lambda ps: ps[:, 0:Cm], up_epi, post_hook=uphook)

                def dn_epi(j, cq, b):
                    slot = 1 if kinds[j] == "s" else 0
                    P.op("dve", lambda e: e.tensor_tensor(out=tmpq[:], in0=psb[b][:, :],
                                                          in1=bc[:, slot, cq * 512:(cq + 1) * 512], op=ALU.mult),
                         reads=["ps%d" % b, "bc%d" % slot], writes=["tmpq"])
                    P.op("dve", lambda e: e.tensor_tensor(out=acc[:, j, cq * 512:(cq + 1) * 512],
                                                           in0=acc[:, j, cq * 512:(cq + 1) * 512], in1=tmpq[:], op=ALU.add),
                         reads=["tmpq", "acc%d" % j], writes=["acc%d" % j])

                hook = None
                if fh == 1 and bi + 1 < len(blocks):
                    nblk = blocks[bi + 1]

                    def hook(cq, nblk=nblk):
                        if cq < len(nblk):
                            ln0_B(nblk[cq], cq)
                        if cq + 1 < len(nblk):
                            ln0_A(nblk[cq + 1])
                stageA(w_down, fh * 4096, 32, main, lambda kc, j: uT[:, kc, j * 128:(j + 1) * 128],
                       lambda kc, j: ["uT%d" % kc], dn_epi, cq_hook=hook)
                if hook is not None:
                    ln0_done.add(bi + 1)
            ln_stats_multi(main, lambda j: acc[:, j, :], lambda j: ["acc%d" % j], eps_t=epsb2, eps_r="epsb2")
            for j in main:
                ar = ["acc%d" % j]
                P.op("act", lambda e, j=j: e.activation(out=acc[:, j, :], in_=acc[:, j, :], func=ACTF.Identity,
                                                        bias=nb[:, j:j + 1], scale=rstd[:, j:j + 1]),
                     reads=ar + ["rstd_%d" % j, "nb_%d" % j], writes=ar)
            for j in main:
                ar = ["acc%d" % j]
                P.op("dve", lambda e, j=j: e.tensor_tensor(out=acc[:, j, :], in0=acc[:, j, :], in1=bc[:, 2, :], op=ALU.mult),
                     reads=ar + ["bc2"], writes=ar)
                P.op("dve", lambda e, j=j: e.tensor_tensor(out=acc[:, j, :], in0=acc[:, j, :], in1=bc[:, 3, :], op=ALU.add),
                     reads=ar + ["bc3"], writes=ar)
                yi = blk[j] - 1
                P.dma("sp", lambda e, j=j, yi=yi: e.dma_start(out=y_all[yi], in_=acc[:, j, :]), reads=ar, writes=["y%d" % yi])

        for bi, blk in enumerate(blocks):
            do_block(bi, blk)
        P.barrier()

        P.run()
    return nc


def _bucket(dist):
    n = np.maximum(dist, 0)
    nf = np.maximum(n, 1).astype(np.float32)
    large = 16 + (np.log(nf / np.float32(16)) / np.float32(math.log(128 / 16)) * np.float32(16)).astype(np.int32)
    large = np.minimum(large, 31)
    return np.where(n < 16, n, large)


def shared_inputs(w_ada, b_ada, w_in, attn_sinks, rel_bias, conv_w, w_o, ln1_g, ln1_b, w_up, w_down, ln2_g, ln2_b):
    f = np.float32
    w_ada = np.asarray(w_ada, f)
    b_ada = np.asarray(b_ada, f)
    w_in = np.asarray(w_in, f)
    rel_bias = np.asarray(rel_bias, f)
    sinks = np.asarray(attn_sinks, f)

    def toB(W, cols_list):
        out = np.empty((len(cols_list), 128, 16, 128), f)
        Wr = W.reshape(16, 128, W.shape[1])
        for m, cols in enumerate(cols_list):
            out[m] = Wr[:, :, cols].transpose(1, 0, 2)
        return out

    voff = [0, 2048, 6144, 8192]
    ada_cols = [np.arange(voff[m // 16] + (m % 16) * 128, voff[m // 16] + (m % 16) * 128 + 128) for m in range(64)]
    sh = {}
    sh["w_adaB"] = toB(w_ada, ada_cols)
    sh["w_adaG"] = np.ascontiguousarray(np.stack([w_ada[:, 4096:6144], w_ada[:, 10240:12288]]))
    sh["b_adaT"] = np.ascontiguousarray(np.stack([b_ada[c] for c in ada_cols], axis=1))
    sh["b_adaG"] = np.ascontiguousarray(np.stack([b_ada[4096:6144], b_ada[10240:12288]]))
    qpairs = [(0, 4), (1, 5), (2, 6), (3, 7), (8, 12), (9, 13), (10, 14), (11, 15)]
    cols = []
    for a, b in qpairs:
        cols.append(np.concatenate([np.arange(a * 64, a * 64 + 64), np.arange(b * 64, b * 64 + 64)]))
    cols.append(np.arange(1024, 1152))
    cols.append(np.arange(1152, 1280))
    for c in range(8):
        cols.append(np.arange(2560 + c * 128, 2560 + (c + 1) * 128))
        cols.append(np.arange(3584 + c * 128, 3584 + (c + 1) * 128))
        cols.append(np.arange(1536 + c * 128, 1536 + (c + 1) * 128))
    sh["w_inB"] = toB(w_in, cols)
    sh["w_kv"] = np.ascontiguousarray(w_in[:, 1024:1536])
    sh["w_o"] = np.asarray(w_o, f)
    sh["w_upB"] = toB(np.asarray(w_up, f), [np.arange(m * 128, (m + 1) * 128) for m in range(64)])
    sh["w_down"] = np.asarray(w_down, f)
    q = np.arange(128)[:, None]
    k = np.arange(256)[None, :]
    dist = q + 128 - k
    valid = (dist >= 0) & (dist < 128)
    bt = rel_bias[_bucket(dist)]
    bt = np.where(valid[:, :, None], bt, f(NEG)).astype(f)
    sh["bias_tab"] = np.ascontiguousarray(bt.transpose(0, 2, 1))
    p = np.arange(128)
    hh = (p // 32) * 4 + (p % 32) // 8
    s = p % 8
    key = np.arange(136)[None, :]
    dist = np.where(key < 128, 128 + s[:, None] - key, s[:, None] - (key - 128))
    valid = (dist >= 0) & (dist < 128)
    sb_ = rel_bias[_bucket(dist), hh[:, None]]
    sh["sbias"] = np.where(valid, sb_, f(NEG)).astype(f)
    sh["sink_bc"] = np.ascontiguousarray(np.broadcast_to(sinks[None, :], (128, 16))).astype(f)
    sh["sink_s"] = np.ascontiguousarray(sinks[hh][:, None]).astype(f)
    sh["conv_wT"] = np.ascontiguousarray(np.asarray(conv_w, f).reshape(3, 8, 128).transpose(2, 1, 0))
    sh["ident"] = np.eye(128, dtype=f)
    sel = np.zeros((128, 2, 64), f)
    for r in range(128):
        hh_ = (r // 32) * 4 + (r % 32) // 8
        sel[r, hh_ % 2, (hh_ // 2) * 8 + (r % 8)] = 1.0
    sh["sel"] = sel
    sh["lnp"] = np.ascontiguousarray(np.stack([np.asarray(a, f) for a in (ln1_g, ln1_b, ln2_g, ln2_b)]))
    return sh


def core_inputs(xp_seq, chunk, npt, x_s, c_p, c_s, ck, cv, stc):
    f = np.float32
    x_all = np.zeros((npt + 2, 128, D), f)
    lo = chunk * npt * 128
    if chunk > 0:
        x_all[0] = xp_seq[lo - 128:lo]
    x_all[1:npt + 1] = xp_seq[lo:lo + npt * 128].reshape(npt, 128, D)
    x_all[npt + 1] = x_s.reshape(128, D)
    cc = np.concatenate([c_p[None, :], c_s], axis=0)
    d = {}
    d["x_all"] = x_all
    d["cT"] = np.ascontiguousarray(cc.reshape(17, 16, 128).transpose(2, 1, 0)).astype(f)
    d["prevmask"] = np.full((128, 128), 0.0 if chunk > 0 else NEG, f)
    d["flag"] = np.full((128, 1), 1.0 if chunk > 0 else 0.0, f)
    d["cache_k"] = np.ascontiguousarray(ck.reshape(16, 128, 256)).astype(f)
    d["cache_v"] = np.ascontiguousarray(cv.reshape(16, 128, 256)).astype(f)
    d["stateT"] = np.ascontiguousarray(stc.reshape(16, 2, 8, 128).transpose(3, 2, 0, 1)).astype(f)
    return d


_NC_CACHE = {}


def kernel(x_prompt, x_sample, cache_k, cache_v, state_conv, c_prompt, c_sample,
           w_ada, b_ada, w_in, attn_sinks, rel_bias, conv_w, w_o, ln1_g, ln1_b,
           w_up, w_down, ln2_g, ln2_b):
    f = np.float32
    x_prompt = np.asarray(x_prompt, f)
    x_sample = np.asarray(x_sample, f)
    cache_k = np.asarray(cache_k, f)
    cache_v = np.asarray(cache_v, f)
    state_conv = np.asarray(state_conv, f)
    c_prompt = np.asarray(c_prompt, f)
    c_sample = np.asarray(c_sample, f)
    B, T = x_prompt.shape[0], x_prompt.shape[1]
    NPT = T // 128 // 4
    sh = shared_inputs(w_ada, b_ada, w_in, attn_sinks, rel_bias, conv_w, w_o, ln1_g, ln1_b, w_up, w_down, ln2_g, ln2_b)
    in_maps = []
    for c in range(8):
        b, chunk = c // 4, c % 4
        d = core_inputs(x_prompt[b], chunk, NPT, x_sample[16 * c:16 * c + 16], c_prompt[b], c_sample[16 * c:16 * c + 16],
                        cache_k[16 * c:16 * c + 16], cache_v[16 * c:16 * c + 16], state_conv[16 * c:16 * c + 16])
        d.update(sh)
        in_maps.append(d)
    if NPT not in _NC_CACHE:
        _NC_CACHE[NPT] = build(NPT)
    nc = _NC_CACHE[NPT]
    res = run_bass_kernel_spmd(nc, in_maps, core_ids=list(range(8))).results
    yp = np.empty((B, T, D), f)
    ys = np.empty((128, 8, D), f)
    for c in range(8):
        b, chunk = c // 4, c % 4
        ya = res[c]["y_all"]
        yp[b, chunk * NPT * 128:(chunk + 1) * NPT * 128] = ya[:NPT].reshape(NPT * 128, D)
        ys[16 * c:16 * c + 16] = ya[NPT].reshape(16, 8, D)
    k_prompt = np.stack([res[3]["kv_last"][:, 0:256], res[7]["kv_last"][:, 0:256]]).reshape(B, 128, 4, 64)
    v_prompt = np.stack([res[3]["kv_last"][:, 256:512], res[7]["kv_last"][:, 256:512]]).reshape(B, 128, 4, 64)
    conv_prompt = np.stack([res[3]["conv_last"], res[7]["conv_last"]]).reshape(B, 2, 1024)
    k_sample = np.concatenate([res[c]["ks_out"] for c in range(8)]).reshape(128, 128, 4, 64)
    v_sample = np.concatenate([res[c]["vs_out"] for c in range(8)]).reshape(128, 128, 4, 64)
    conv_sample = np.concatenate([res[c]["convs_out"].reshape(16, 2, 1024) for c in range(8)])
    return (yp, ys, k_prompt.astype(f), v_prompt.astype(f), conv_prompt.astype(f),
            k_sample.astype(f), v_sample.astype(f), conv_sample.astype(f))
```
